# Optimizing a Trainium2 kernel written in Bass

```python
import jax, jax.numpy as jnp
from jax import lax
import numpy as np

D_MODEL = 1024
BATCH = 8
SEQ = 8192
DEPTH = 1
DEC_BATCH = 2
DEC_SEQ = 16384
PAST_LEN = 128

N_MEM = 256
EPS = 1e-6
ROPE_THETA = 10000.0
Q_BLOCK = 128
MLA_HEADS = 8
MLA_NOPE = 64
MLA_ROPE = 32
MLA_V = 64
MLA_QK = MLA_NOPE + MLA_ROPE
Q_LORA = 384
KV_LORA = 256
SWA_HEADS = 8
SWA_KV_HEADS = 2
SWA_HD = 64
WINDOW = 128
BLOCK = 128
MEM_HEADS = 4
MEM_HD = 128
N_BRANCH = 3
D_FF = 2816
CONV_W = 3
SPLITS = (Q_LORA, KV_LORA, MLA_ROPE, SWA_HEADS * SWA_HD, SWA_KV_HEADS * SWA_HD, SWA_KV_HEADS * SWA_HD, MEM_HEADS * MEM_HD, N_BRANCH * D_MODEL)
D_IN = Q_LORA + KV_LORA + MLA_ROPE + SWA_HEADS * SWA_HD + 2 * SWA_KV_HEADS * SWA_HD + MEM_HEADS * MEM_HD + N_BRANCH * D_MODEL

kernel_name = 'hybrid_mla_swa_mem_encoder'


def rmsnorm(x, g):
    xf = x.astype(jnp.float32)
    y = xf * lax.rsqrt(jnp.mean(xf * xf, axis=-1, keepdims=True) + EPS)
    return (y * g.astype(jnp.float32)).astype(x.dtype)


def rope_tables(seq, dim):
    inv_freq = 1.0 / (ROPE_THETA ** (jnp.arange(0, dim, 2, dtype=jnp.float32) / dim))
    ang = jnp.arange(seq, dtype=jnp.float32)[:, None] * inv_freq[None, :]
    return jnp.cos(ang), jnp.sin(ang)


def apply_rope(x, cos, sin):
    xf = x.astype(jnp.float32)
    x1, x2 = jnp.split(xf, 2, axis=-1)
    c = cos[None, :, None, :]
    s = sin[None, :, None, :]
    return jnp.concatenate([x1 * c - x2 * s, x2 * c + x1 * s], axis=-1).astype(x.dtype)


def dense_attention_blocked(q, k, v, scale):
    b, s, h, dq = q.shape
    dv = v.shape[-1]
    nb = s // Q_BLOCK
    qb = q.reshape(b, nb, Q_BLOCK, h, dq).transpose(1, 0, 2, 3, 4)

    def one_block(qi):
        sc = jnp.einsum('bqhd,bkhd->bhqk', qi, k, preferred_element_type=jnp.float32) * scale
        p = jax.nn.softmax(sc, axis=-1)
        return jnp.einsum('bhqk,bkhd->bqhd', p.astype(v.dtype), v)

    o = lax.map(one_block, qb)
    return o.transpose(1, 0, 2, 3, 4).reshape(b, s, h * dv)


def mla_branch(c_q, c_kv, k_rope, q_a_norm, w_q_b, kv_a_norm, w_kv_b, g_q, g_k):
    b, s, _ = c_q.shape
    q = (rmsnorm(c_q, q_a_norm) @ w_q_b).reshape(b, s, MLA_HEADS, MLA_QK)
    kv = (rmsnorm(c_kv, kv_a_norm) @ w_kv_b).reshape(b, s, MLA_HEADS, MLA_NOPE + MLA_V)
    k_nope, v = kv[..., :MLA_NOPE], kv[..., MLA_NOPE:]
    k = jnp.concatenate([k_nope, jnp.broadcast_to(k_rope[:, :, None, :], (b, s, MLA_HEADS, MLA_ROPE))], axis=-1)
    q = rmsnorm(q, g_q)
    k = rmsnorm(k, g_k)
    cos, sin = rope_tables(s, MLA_ROPE)
    q = jnp.concatenate([q[..., :MLA_NOPE], apply_rope(q[..., MLA_NOPE:], cos, sin)], axis=-1)
    k = jnp.concatenate([k[..., :MLA_NOPE], apply_rope(k[..., MLA_NOPE:], cos, sin)], axis=-1)
    return dense_attention_blocked(q, k, v, MLA_QK ** -0.5)


def swa_branch(q, k, v, g_q, g_k, sink):
    b, s, _ = q.shape
    grp = SWA_HEADS // SWA_KV_HEADS
    nb = s // BLOCK
    q = rmsnorm(q.reshape(b, s, SWA_HEADS, SWA_HD), g_q)
    k = rmsnorm(k.reshape(b, s, SWA_KV_HEADS, SWA_HD), g_k)
    v = v.reshape(b, s, SWA_KV_HEADS, SWA_HD)
    cos, sin = rope_tables(s, SWA_HD)
    q = apply_rope(q, cos, sin)
    k = apply_rope(k, cos, sin)

    def windows(t):
        tp = jnp.pad(t, ((0, 0), (BLOCK, BLOCK), (0, 0), (0, 0))).reshape(b, nb + 2, BLOCK, SWA_KV_HEADS, SWA_HD)
        return jnp.concatenate([tp[:, :-2], tp[:, 1:-1], tp[:, 2:]], axis=2)

    kw, vw = windows(k), windows(v)
    qb = q.reshape(b, nb, BLOCK, SWA_KV_HEADS, grp, SWA_HD)
    sc = jnp.einsum('bnqhgd,bnkhd->bnhgqk', qb, kw, preferred_element_type=jnp.float32) * (SWA_HD ** -0.5)
    qpos = jnp.arange(nb)[:, None, None] * BLOCK + jnp.arange(BLOCK)[None, :, None]
    kpos = jnp.arange(nb)[:, None, None] * BLOCK - BLOCK + jnp.arange(3 * BLOCK)[None, None, :]
    mask = (jnp.abs(qpos - kpos) <= WINDOW) & (kpos >= 0) & (kpos < s)
    sc = jnp.where(mask[None, :, None, None], sc, -jnp.inf)
    sk = sink.astype(jnp.float32).reshape(SWA_KV_HEADS, grp)[None, None, :, :, None, None]
    m = jnp.maximum(jnp.max(sc, axis=-1, keepdims=True), sk)
    p = jnp.exp(sc - m)
    p = p / (jnp.sum(p, axis=-1, keepdims=True) + jnp.exp(sk - m))
    o = jnp.einsum('bnhgqk,bnkhd->bnqhgd', p.astype(vw.dtype), vw)
    return o.reshape(b, s, SWA_HEADS * SWA_HD)


def mem_branch(q, mem_n, w_mem_kv, g_q, g_k):
    b, s, _ = q.shape
    n_mem = mem_n.shape[1]
    q = rmsnorm(q.reshape(b, s, MEM_HEADS, MEM_HD), g_q)
    kv = (mem_n @ w_mem_kv).reshape(b, n_mem, 2, MEM_HEADS, MEM_HD)
    k = rmsnorm(kv[:, :, 0], g_k)
    v = kv[:, :, 1]
    sc = jnp.einsum('bshd,bmhd->bhsm', q, k, preferred_element_type=jnp.float32) * (MEM_HD ** -0.5)
    p = jax.nn.softmax(sc, axis=-1)
    o = jnp.einsum('bhsm,bmhd->bshd', p.astype(v.dtype), v)
    return o.reshape(b, s, MEM_HEADS * MEM_HD)


def depthwise_conv_centred(u, w, bias):
    c = u.shape[-1]
    y = lax.conv_general_dilated(u, w[:, None, :].astype(u.dtype), window_strides=(1,),
                                 padding=((CONV_W // 2, CONV_W // 2),),
                                 dimension_numbers=('NWC', 'WIO', 'NWC'), feature_group_count=c)
    return y + bias.astype(u.dtype)


def encoder_layer(x, mem, g_mix, g_mem, w_in, q_a_norm, w_q_b, kv_a_norm, w_kv_b, g_q_mla, g_k_mla,
                  g_q_swa, g_k_swa, swa_sink, w_mem_kv, g_q_mem, g_k_mem, w_o_mla, w_o_swa, w_o_mem,
                  w_out, g_ffn, w_up, conv_w, conv_b, w_down):
    b, s, d = x.shape
    h = rmsnorm(x, g_mix)
    z = h @ w_in
    offs = np.cumsum(SPLITS)[:-1].tolist()
    c_q, c_kv, k_rope, q_s, k_s, v_s, q_m, gate_logits = jnp.split(z, offs, axis=-1)
    o_mla = mla_branch(c_q, c_kv, k_rope, q_a_norm, w_q_b, kv_a_norm, w_kv_b, g_q_mla, g_k_mla)
    o_swa = swa_branch(q_s, k_s, v_s, g_q_swa, g_k_swa, swa_sink)
    o_mem = mem_branch(q_m, rmsnorm(mem, g_mem), w_mem_kv, g_q_mem, g_k_mem)
    gates = jax.nn.sigmoid(gate_logits.astype(jnp.float32)).reshape(b, s, N_BRANCH, d)
    merged = (gates[:, :, 0] * (o_mla @ w_o_mla) + gates[:, :, 1] * (o_swa @ w_o_swa)
              + gates[:, :, 2] * (o_mem @ w_o_mem)).astype(x.dtype)
    x = x + merged @ w_out
    h2 = rmsnorm(x, g_ffn)
    u = depthwise_conv_centred(h2 @ w_up, conv_w, conv_b)
    a, val = jnp.split(u, 2, axis=-1)
    return x + (jax.nn.silu(a) * val) @ w_down


def setup_inputs(seed: int = 0) -> dict:
    key = jax.random.key(seed)
    ks = jax.random.split(key, 32)
    L = DEPTH

    def dense(k, shape, fan_in):
        return jax.random.normal(k, shape, jnp.float32) * (fan_in ** -0.5)

    def gain(k, shape):
        return 1.0 + 0.02 * jax.random.normal(k, shape, jnp.float32)

    return {
        'x_prompt': jax.random.normal(ks[0], (BATCH, SEQ, D_MODEL), jnp.float32),
        'x_sample': jax.random.normal(ks[1], (DEC_BATCH, DEC_SEQ, D_MODEL), jnp.float32),
        'mem_prompt': jax.random.normal(ks[2], (BATCH, N_MEM, D_MODEL), jnp.float32),
        'mem_sample': jax.random.normal(ks[3], (DEC_BATCH, N_MEM, D_MODEL), jnp.float32),
        'g_mix': gain(ks[4], (L, D_MODEL)),
        'g_mem': gain(ks[5], (L, D_MODEL)),
        'w_in': dense(ks[6], (L, D_MODEL, D_IN), D_MODEL),
        'q_a_norm': gain(ks[7], (L, Q_LORA)),
        'w_q_b': dense(ks[8], (L, Q_LORA, MLA_HEADS * MLA_QK), Q_LORA),
        'kv_a_norm': gain(ks[9], (L, KV_LORA)),
        'w_kv_b': dense(ks[10], (L, KV_LORA, MLA_HEADS * (MLA_NOPE + MLA_V)), KV_LORA),
        'g_q_mla': gain(ks[11], (L, MLA_QK)),
        'g_k_mla': gain(ks[12], (L, MLA_QK)),
        'g_q_swa': gain(ks[13], (L, SWA_HD)),
        'g_k_swa': gain(ks[14], (L, SWA_HD)),
        'swa_sink': 0.5 * jax.random.normal(ks[15], (L, SWA_HEADS), jnp.float32),
        'w_mem_kv': dense(ks[16], (L, D_MODEL, 2 * MEM_HEADS * MEM_HD), D_MODEL),
        'g_q_mem': gain(ks[17], (L, MEM_HD)),
        'g_k_mem': gain(ks[18], (L, MEM_HD)),
        'w_o_mla': dense(ks[19], (L, MLA_HEADS * MLA_V, D_MODEL), MLA_HEADS * MLA_V),
        'w_o_swa': dense(ks[20], (L, SWA_HEADS * SWA_HD, D_MODEL), SWA_HEADS * SWA_HD),
        'w_o_mem': dense(ks[21], (L, MEM_HEADS * MEM_HD, D_MODEL), MEM_HEADS * MEM_HD),
        'w_out': dense(ks[22], (L, D_MODEL, D_MODEL), D_MODEL),
        'g_ffn': gain(ks[23], (L, D_MODEL)),
        'w_up': dense(ks[24], (L, D_MODEL, 2 * D_FF), D_MODEL),
        'conv_w': dense(ks[25], (L, CONV_W, 2 * D_FF), CONV_W),
        'conv_b': 0.02 * jax.random.normal(ks[26], (L, 2 * D_FF), jnp.float32),
        'w_down': dense(ks[27], (L, D_FF, D_MODEL), D_FF),
    }


def reference(x_prompt, x_sample, mem_prompt, mem_sample, g_mix, g_mem, w_in, q_a_norm, w_q_b,
              kv_a_norm, w_kv_b, g_q_mla, g_k_mla, g_q_swa, g_k_swa, swa_sink, w_mem_kv, g_q_mem,
              g_k_mem, w_o_mla, w_o_swa, w_o_mem, w_out, g_ffn, w_up, conv_w, conv_b, w_down):
    weights = (g_mix, g_mem, w_in, q_a_norm, w_q_b, kv_a_norm, w_kv_b, g_q_mla, g_k_mla, g_q_swa,
               g_k_swa, swa_sink, w_mem_kv, g_q_mem, g_k_mem, w_o_mla, w_o_swa, w_o_mem, w_out,
               g_ffn, w_up, conv_w, conv_b, w_down)
    y_prompt = x_prompt
    y_sample = x_sample
    for layer in range(DEPTH):
        lw = [w[layer] for w in weights]
        y_prompt = encoder_layer(y_prompt, mem_prompt, *lw)
        y_sample = encoder_layer(y_sample, mem_sample, *lw)
    return (y_prompt, y_sample)
```

```python
import contextlib
import numpy as np
import ml_dtypes
import concourse.bass as bass
import concourse.mybir as mybir
from concourse.bass_utils import run_bass_kernel_spmd

F32 = mybir.dt.float32
BF16 = mybir.dt.bfloat16
ALU = mybir.AluOpType
AF = mybir.ActivationFunctionType
AX = mybir.AxisListType

EPOCH = 30000
N_DMA_SEMS = 40
N_SP_SEMS = 24
EPS = 1e-6
D = 1024
D_IN = 5024
D_FF = 2816
NCH_FF = 44
N_MEM = 256
NEG = -30000.0


class Op:
    __slots__ = ("eng", "fn", "deps", "is_dma", "sem", "val", "need_inc", "cnt")

    def __init__(self, eng, fn, is_dma):
        self.eng = eng
        self.fn = fn
        self.deps = []
        self.is_dma = is_dma
        self.sem = None
        self.val = None
        self.need_inc = False
        self.cnt = None


class Buf:
    __slots__ = ("name", "last_w", "readers", "dma_readers", "excl")

    def __init__(self, name, excl=False):
        self.name = name
        self.excl = excl
        self.last_w = None
        self.readers = {}
        self.dma_readers = []


class Prog:
    ENGS = ("pe", "act", "dve", "pool", "sp")

    def __init__(self, nc):
        self.nc = nc
        self.ops = {e: [] for e in self.ENGS}
        self.dma_rr_e = {e: 0 for e in self.ENGS}
        self.dma_last = [None] * N_DMA_SEMS
        self.dma_count = [0] * N_DMA_SEMS
        self.pending = {e: [] for e in self.ENGS}
        self.cap = None
        self.est = 0.3

    def replay_interleaved(self, lists):
        items = []
        for ci, l in enumerate(lists):
            wfin = {}
            rfin = {}
            for k, it in enumerate(l):
                if it[1] == "add":
                    reads, writes = it[4], it[5]
                else:
                    reads, writes = it[5], it[6]
                t = 0.0
                for b in reads:
                    t = max(t, wfin.get(id(b), 0.0))
                    if b.excl:
                        t = max(t, rfin.get(id(b), 0.0))
                for b in writes:
                    t = max(t, wfin.get(id(b), 0.0), rfin.get(id(b), 0.0))
                fin = t + it[0]
                for b in reads:
                    rfin[id(b)] = max(rfin.get(id(b), 0.0), fin)
                    if b.excl:
                        wfin[id(b)] = max(wfin.get(id(b), 0.0), fin)
                for b in writes:
                    wfin[id(b)] = max(wfin.get(id(b), 0.0), fin)
                items.append((t, ci, k, it))
        items.sort(key=lambda x: (x[0], x[1], x[2]))
        for _, _, _, it in items:
            if it[1] == "add":
                self.add(*it[2:])
            else:
                self.dma(it[2], it[3], it[4], reads=it[5], writes=it[6], slow=it[7])

    def barrier(self):
        deps = []
        for e in self.ENGS:
            for op in reversed(self.ops[e]):
                if not op.is_dma:
                    deps.append(op)
                    break
        deps += [d for d in self.dma_last if d is not None]
        for d in deps:
            d.need_inc = True
        for e in self.ENGS:
            self.pending[e] = list(deps)

    def _deps(self, op, reads, writes):
        ex = [b for b in reads if b.excl]
        if ex:
            reads = [b for b in reads if not b.excl]
            writes = list(writes) + [b for b in ex if b not in writes]
        deps = []
        for b in reads:
            if b.last_w is not None:
                deps.append((b.last_w, "raw"))
        for b in writes:
            if b.last_w is not None:
                deps.append((b.last_w, "waw"))
            for r in b.readers.values():
                deps.append((r, "war"))
            for r in b.dma_readers:
                deps.append((r, "war"))
        out = []
        seen = set()
        for d, kind in deps:
            if d is op or id(d) in seen:
                continue
            if (not d.is_dma) and (not op.is_dma) and d.eng == op.eng:
                if op.eng == "pe":
                    continue
            seen.add(id(d))
            out.append(d)
        if self.pending[op.eng]:
            for d in self.pending[op.eng]:
                if id(d) not in seen and not (d.eng == op.eng and not d.is_dma and not op.is_dma):
                    seen.add(id(d))
                    out.append(d)
            self.pending[op.eng] = []
        op.deps = out
        for d in out:
            d.need_inc = True
        for b in reads:
            if op.is_dma:
                b.dma_readers.append(op)
            else:
                b.readers[op.eng] = op
        for b in writes:
            b.last_w = op
            b.readers = {}
            b.dma_readers = []

    def add(self, eng, fn, reads=(), writes=()):
        if self.cap is not None:
            self.cap.append((self.est, "add", eng, fn, reads, writes))
            return None
        op = Op(eng, fn, False)
        self._deps(op, reads, writes)
        self.ops[eng].append(op)
        return op

    def dma(self, eng, out, in_, reads=(), writes=(), slow=False):
        if self.cap is not None:
            self.cap.append((1.5, "dma", eng, out, in_, reads, writes, slow))
            return None

        def fn(e):
            if slow:
                return e.dma_start(out=out, in_=in_, allow_slow_non_contiguous=True)
            return e.dma_start(out=out, in_=in_)
        op = Op(eng, fn, True)
        self._deps(op, reads, writes)
        lo, hi = (0, N_SP_SEMS) if eng == "sp" else (N_SP_SEMS, N_DMA_SEMS)
        s = lo + self.dma_rr_e[eng] % (hi - lo)
        self.dma_rr_e[eng] += 1
        prev = self.dma_last[s]
        if prev is not None and prev not in op.deps:
            op.deps.append(prev)
        self.dma_count[s] += 1
        op.sem = ("dma", s)
        op.val = 16 * self.dma_count[s]
        self.dma_last[s] = op
        self.ops[eng].append(op)
        return op

    def emit(self):
        nc = self.nc
        n_ep = {}
        for e in self.ENGS:
            c = 0
            for op in self.ops[e]:
                if not op.is_dma and op.need_inc:
                    c += 1
                    op.cnt = c
                    op.sem = (e, (c - 1) // EPOCH)
                    op.val = (c - 1) % EPOCH + 1
            n_ep[e] = max(1, (c + EPOCH - 1) // EPOCH)
        with contextlib.ExitStack() as st:
            sems = {}
            for e in self.ENGS:
                for k in range(n_ep[e]):
                    sems[(e, k)] = st.enter_context(nc.semaphore(f"s_{e}_{k}"))
            for s in range(N_DMA_SEMS):
                sems[("dma", s)] = st.enter_context(nc.semaphore(f"s_dma_{s}"))
            block = st.enter_context(nc.Block())
            engmap = {"pe": block.tensor, "act": block.scalar, "dve": block.vector,
                      "pool": block.gpsimd, "sp": block.sync}

            def make(ename):
                ops = self.ops[ename]

                def body(e):
                    waited = {}
                    for op in ops:
                        for d in op.deps:
                            if waited.get(d.sem, 0) >= d.val:
                                continue
                            e.wait_ge(sems[d.sem], d.val)
                            waited[d.sem] = d.val
                            if d.sem[0] != "dma":
                                for k in range(d.sem[1]):
                                    waited[(d.sem[0], k)] = EPOCH
                        inst = op.fn(e)
                        if op.is_dma:
                            inst.then_inc(sems[op.sem], 16)
                        elif op.need_inc:
                            inst.then_inc(sems[op.sem], 1)
                    if ename == "sp":
                        for s in range(N_DMA_SEMS):
                            if self.dma_count[s]:
                                e.wait_ge(sems[("dma", s)], 16 * self.dma_count[s])
                return body

            for ename in self.ENGS:
                if self.ops[ename] or ename == "sp":
                    engmap[ename](make(ename))


class T:
    def __init__(self, h, name, excl=False):
        self.h = h
        self.b = Buf(name, excl)

    def __getitem__(self, k):
        return self.h[k]


def build_program(SP, SS, FT):
    QS = SS // 4
    jobs = [
        dict(S=SP, KB=SP // 128, qoff=0, NQ=SP // 128, f0=0, fn=SP // 128, qg0=0),
        dict(S=SS, KB=SS // 128, qoff=1, NQ=QS // 128 + 2, f0=1, fn=QS // 128, qg0=SP // 128),
    ]
    NQT = sum(j["NQ"] for j in jobs)
    nc = bass.Bass("TRN2", target_bir_lowering=False)

    def din(name, shape, dt=F32):
        return nc.dram_tensor(name, list(shape), dt, kind="ExternalInput").ap()

    def dscr(name, shape, dt=BF16):
        return nc.dram_tensor(name, list(shape), dt, kind="Internal").ap()

    xkv = [din("xkv0", [SP, D]), din("xkv1", [SS, D])]
    mem = [din("mem0", [N_MEM, D]), din("mem1", [N_MEM, D])]
    rope_d = [din(f"rope{j}", [jobs[j]["KB"], 128, 96]) for j in range(2)]
    hval_d = din("hval", [128, NQT])
    swab_d = din("swab", [128, NQT * 3])
    bandm_d = din("bandm", [128, 2, 128], BF16)
    ident_d = din("ident", [128, 128], BF16)
    w_in_d = din("w_in", [D, D_IN])
    g_mix_d = din("g_mix", [128, 8])
    g_mem_d = din("g_mem", [128, 8])
    g_ffn_d = din("g_ffn", [128, 8])
    qan_d = din("q_a_norm", [128, 3])
    kvan_d = din("kv_a_norm", [128, 2])
    w_qb_d = din("w_q_b", [384, 768])
    w_kvb_d = din("w_kv_b", [256, 1024])
    gqm_d = din("g_q_mla", [1, 96])
    gkm_d = din("g_k_mla", [1, 96])
    gqs_d = din("g_q_swa", [1, 64])
    gks_d = din("g_k_swa", [1, 64])
    gqe_d = din("g_q_mem", [1, 128])
    gke_d = din("g_k_mem", [1, 128])
    sink_d = din("swa_sink", [1, 8])
    w_memkv_d = din("w_mem_kv", [D, 1024])
    w_omla_d = din("w_o_mla", [512, D])
    w_oswa_d = din("w_o_swa", [512, D])
    w_omem_d = din("w_o_mem", [512, D])
    w_out_d = din("w_out", [D, D])
    w_up_d = din("w_up", [D, 2 * D_FF])
    convw_d = din("conv_w", [128, NCH_FF, 3])
    convb_d = din("conv_b", [128, NCH_FF])
    w_down_d = din("w_down", [D_FF, D])
    y_d = [nc.dram_tensor("y0", [SP, D], F32, kind="ExternalOutput").ap(),
           nc.dram_tensor("y1", [QS, D], F32, kind="ExternalOutput").ap()]
    KT_d = [dscr(f"KT{j}", [8, 96, jobs[j]["S"]]) for j in range(2)]
    V_d = [dscr(f"V{j}", [8, 128, jobs[j]["KB"], 128]) for j in range(2)]
    KsT_d = [dscr(f"KsT{j}", [2, 64, (jobs[j]["KB"] + 2) * 128]) for j in range(2)]
    Vs_d = [dscr(f"Vs{j}", [jobs[j]["KB"] + 2, 128, 256]) for j in range(2)]
    QT_d = [dscr(f"QT{j}", [8, 96, jobs[j]["NQ"] * 128]) for j in range(2)]
    OT_d = [dscr(f"OT{j}", [4, 128, jobs[j]["NQ"] * 128]) for j in range(2)]
    OS_d = [dscr(f"OS{j}", [4, 128, jobs[j]["NQ"] * 128]) for j in range(2)]
    OM_d = [dscr(f"OM{j}", [4, 128, jobs[j]["NQ"] * 128]) for j in range(2)]
    XN_d = [dscr(f"XN{j}", [jobs[j]["NQ"] * 128, D], F32) for j in range(2)]
    H2_d = [dscr(f"H2{j}", [8, 128, jobs[j]["NQ"] * 128 + 2]) for j in range(2)]
    HT_d = [dscr(f"HT{j}", [8, 128, jobs[j]["NQ"] * 128]) for j in range(2)]
    dbuf = {}

    def DB(*key):
        return Buf("scratch")

    P = Prog(nc)
    root = contextlib.ExitStack()

    def sbuf(st, name, shape, dt):
        return T(st.enter_context(nc.sbuf_tensor("sb_" + name, list(shape), dt)), name)

    def bl(ts_):
        return [t if isinstance(t, Buf) else t.b for t in ts_]

    def fsz(ap):
        n = 1
        for d_ in ap.shape[1:]:
            n *= int(d_)
        return n

    def mm(out, lhsT, rhs, start, stop, r, w):
        P.est = 0.03 + fsz(rhs) / 2400.0 * (4.0 if rhs.dtype == F32 else 1.0)
        P.add("pe", lambda e: e.matmul(out, lhsT, rhs, start=start, stop=stop), reads=bl(r), writes=bl(w))

    def tr(out, in_, idn, r, w):
        P.est = 0.07
        P.add("pe", lambda e: e.transpose(out, in_, idn), reads=bl(r), writes=bl(w))

    def vest(eng, out):
        n = fsz(out)
        return (0.35 + n / 480.0) if eng == "pool" else (0.2 + n / 900.0)

    def tt(eng, out, in0, in1, op, r, w):
        P.est = vest(eng, out)
        P.add(eng, lambda e: e.tensor_tensor(out, in0, in1, op), reads=bl(r), writes=bl(w))

    def ts(eng, out, in0, s1, s2, op0, op1, r, w):
        P.est = vest(eng, out)
        if op1 is None:
            P.add(eng, lambda e: e.tensor_scalar(out, in0, s1, None, op0), reads=bl(r), writes=bl(w))
        else:
            P.add(eng, lambda e: e.tensor_scalar(out, in0, s1, s2, op0, op1), reads=bl(r), writes=bl(w))

    def stt(eng, out, in0, sc, in1, op0, op1, r, w):
        P.est = vest(eng, out)
        P.add(eng, lambda e: e.scalar_tensor_tensor(out, in0, sc, in1, op0, op1), reads=bl(r), writes=bl(w))

    def cp(eng, out, in_, r, w):
        P.est = (0.3 + fsz(out) / 1150.0) if eng == "act" else vest(eng, out)
        if eng == "act":
            P.add("act", lambda e: e.activation(out, in_, AF.Copy), reads=bl(r), writes=bl(w))
        else:
            P.add(eng, lambda e: e.tensor_copy(out, in_), reads=bl(r), writes=bl(w))

    def act(out, in_, func, r, w, **kw):
        P.est = 0.3 + fsz(out) / 1150.0
        P.add("act", lambda e: e.activation(out, in_, func, **kw), reads=bl(r), writes=bl(w))

    def memset(eng, ap, val, w):
        P.add(eng, lambda e: e.memset(ap, val), writes=bl(w))

    def red(out, in_, r, w):
        P.est = 0.2 + fsz(in_) / 900.0
        P.add("dve", lambda e: e.tensor_reduce(out, in_, AX.X, ALU.add), reads=bl(r), writes=bl(w))

    def recip(out, in_, r, w):
        P.est = 0.3 + fsz(out) * 0.0048
        P.add("dve", lambda e: e.reciprocal(out, in_), reads=bl(r), writes=bl(w))

    def dma(eng, out, in_, r, w, slow=False):
        P.dma(eng, out, in_, reads=bl(r), writes=bl(w), slow=slow)

    with root:
        PD = [root.enter_context(nc.psum_tensor(f"pd{i}", [128, 1024], F32)) for i in range(4)]

        class PBank:
            def __init__(self, i):
                self.h = PD[i // 2]
                self.off = (i % 2) * 512
                self.b = Buf(f"pb{i}", True)

            def __getitem__(self, k):
                return self.h[:, self.off:self.off + 512][k]

        PB = [PBank(i) for i in range(8)]

        def pbf(i):
            return PB[i][:].bitcast(BF16)

        ident = sbuf(root, "ident", [128, 128], BF16)
        bandm = sbuf(root, "bandm", [128, 2, 128], BF16)
        hval = sbuf(root, "hval", [128, NQT], F32)
        swab = sbuf(root, "swab", [128, NQT * 3], F32)
        e128 = sbuf(root, "e128", [1, 128], F32)
        sinkrow = sbuf(root, "sinkrow", [1, 8 * 128], F32)
        sink_t = sbuf(root, "sink_t", [1, 8], F32)
        gt = {}
        for nm, dd, w_ in (("gqm", gqm_d, 96), ("gkm", gkm_d, 96), ("gqs", gqs_d, 64), ("gks", gks_d, 64),
                           ("gqe", gqe_d, 128), ("gke", gke_d, 128)):
            gt[nm] = sbuf(root, nm, [128, w_], F32)
            dma("sp", gt[nm][:], dd.partition_broadcast(128), [], [gt[nm]])
        dma("sp", ident[:], ident_d, [], [ident])
        dma("sp", bandm[:], bandm_d, [], [bandm])
        dma("sp", hval[:], hval_d, [], [hval])
        dma("sp", swab[:], swab_d, [], [swab])
        dma("sp", sink_t[:], sink_d, [], [sink_t])
        memset("dve", e128[:], 0.0, [e128])
        memset("dve", e128[0:1, 64:128], 1.0, [e128])
        act(sink_t[:], sink_t[:], AF.Exp, [sink_t], [sink_t])
        cp("dve", sinkrow[:].rearrange("p (h t) -> p h t", h=8), sink_t[:].unsqueeze(2).to_broadcast([1, 8, 128]),
           [sink_t], [sinkrow])

        def load_weight(w, src, nchunk, ncols, stage, gain=None, pk=128, cpiece=None):
            cpiece = cpiece or ncols
            stages = stage if isinstance(stage, list) else [stage]
            k = 0
            for c in range(nchunk):
                for c0 in range(0, ncols, cpiece):
                    n = min(cpiece, ncols - c0)
                    st_ = stages[k % len(stages)]
                    k += 1
                    dma("sp", st_[0:pk, 0:n], src[c * pk:(c + 1) * pk, c0:c0 + n], [], [st_])
                    if gain is not None:
                        ts("dve", w[:, c, c0:c0 + n], st_[0:pk, 0:n], gain[:, c:c + 1], None, ALU.mult, None, [st_, gain], [w])
                    else:
                        cp("dve", w[:, c, c0:c0 + n], st_[0:pk, 0:n], [st_], [w])

        I32 = mybir.dt.int32

        def rstd_dve(ssq, n, dim, rstd, tn):
            v = ssq[:, 0:n]
            y = rstd[:, 0:n]
            t_ = tn[:, 0:n]
            ts("dve", v, v, 1.0 / dim, EPS, ALU.mult, ALU.add, [ssq], [ssq])
            P.add("dve", lambda e: e.tensor_single_scalar(y.bitcast(I32), v.bitcast(I32), 1, ALU.logical_shift_right),
                  reads=[ssq.b], writes=[rstd.b])
            P.add("dve", lambda e: e.tensor_scalar(y.bitcast(I32), y.bitcast(I32), -1, 0x5f3759df, ALU.mult, ALU.add),
                  reads=[rstd.b], writes=[rstd.b])
            for _ in range(2):
                tt("dve", t_, y, y, ALU.mult, [rstd], [tn])
                tt("dve", t_, t_, v, ALU.mult, [tn, ssq], [tn])
                ts("dve", t_, t_, -0.5, 1.5, ALU.mult, ALU.add, [tn], [tn])
                tt("dve", y, y, t_, ALU.mult, [rstd, tn], [rstd])

        def rstd_from_ssq(ssq, n, dim, rstd):
            act(rstd[:, 0:n], ssq[:, 0:n], AF.Ln, [ssq], [rstd], scale=1.0 / dim, bias=EPS)
            act(rstd[:, 0:n], rstd[:, 0:n], AF.Exp, [rstd], [rstd], scale=-0.5)

        def transposes(srcs, cols, pbank, dst_ap, dst_t, src_ts, evac="dve", slot0=0):
            n = len(srcs)
            pv = pbf(pbank).rearrange("p (c t) -> p c t", c=8)
            for c in range(n):
                tr(pv[0:cols, slot0 + c, :], srcs[c], ident[:], src_ts + [ident], [PB[pbank]])
            cp(evac, dst_ap, pv[0:cols, slot0:slot0 + n, :], [PB[pbank]], [dst_t])

        def headnorm(src_ap, src_ts, H, Dh, gain, out_ap, out_t, tmp, sq, ssq, rstd, rope=None):
            tv = tmp[:, 0:H, 0:Dh]
            sv = sq[:, 0:H, 0:Dh]
            act(sv, src_ap, AF.Square, src_ts, [sq])
            red(ssq[:, 0:H], sv, [sq], [ssq])
            rstd_from_ssq(ssq, H, Dh, rstd)
            tt("dve", tv, src_ap, rstd[:, 0:H].unsqueeze(2).to_broadcast([128, H, Dh]), ALU.mult, src_ts + [rstd], [tmp])
            gb = gain[:, 0:Dh].unsqueeze(1).to_broadcast([128, H, Dh])
            if rope is None:
                tt("dve", out_ap, tv, gb, ALU.mult, [tmp, gain], [out_t])
                return
            tab, r, tab_t = rope
            n0 = Dh - 2 * r
            tt("dve", tv, tv, gb, ALU.mult, [tmp, gain], [tmp])
            if n0 > 0:
                cp("dve", out_ap[:, :, 0:n0], tmp[:, 0:H, 0:n0], [tmp], [out_t])
            x1 = tmp[:, 0:H, n0:n0 + r]
            x2 = tmp[:, 0:H, n0 + r:Dh]
            cb = tab[:, 0:r].unsqueeze(1).to_broadcast([128, H, r])
            sb_ = tab[:, r:2 * r].unsqueeze(1).to_broadcast([128, H, r])
            a = sq[:, 0:H, 0:r]
            b = sq[:, 0:H, r:2 * r]
            c_ = sq[:, 0:H, 2 * r:3 * r]
            d_ = sq[:, 0:H, 3 * r:4 * r]
            tt("dve", a, x1, cb, ALU.mult, [tmp, tab_t], [sq])
            tt("dve", b, x2, sb_, ALU.mult, [tmp, tab_t], [sq])
            tt("dve", c_, x2, cb, ALU.mult, [tmp, tab_t], [sq])
            tt("dve", d_, x1, sb_, ALU.mult, [tmp, tab_t], [sq])
            tt("dve", out_ap[:, :, n0:n0 + r], a, b, ALU.subtract, [sq], [out_t])
            tt("dve", out_ap[:, :, n0 + r:Dh], c_, d_, ALU.add, [sq], [out_t])

        def rms_block(x_t, junk, ssq, rstd, h, extra_scale=None):
            act(junk[:], x_t[:], AF.Square, [x_t], [junk, ssq], accum_out=ssq[:, 0:1])
            rstd_from_ssq(ssq, 1, D, rstd)
            if extra_scale is not None:
                tt("dve", rstd[:, 0:1], rstd[:, 0:1], extra_scale, ALU.mult, [rstd, hval], [rstd])
            ts("dve", h[:], x_t[:], rstd[:, 0:1], None, ALU.mult, None, [x_t, rstd], [h])

        def pipeline(n, levels):
            nl = len(levels)
            for i in range(n + nl - 1):
                lists = []
                for k in range(nl - 1, -1, -1):
                    b_ = i - k
                    if 0 <= b_ < n:
                        for fn in levels[k]:
                            P.cap = []
                            fn(b_)
                            lists.append(P.cap)
                            P.cap = None
                P.replay_interleaved(lists)

        def load_cols(w, dst0, src, col0, ncols, nchunk, stage, gain):
            stages = stage if isinstance(stage, list) else [stage]
            for c in range(nchunk):
                st_ = stages[c % len(stages)]
                dma("sp", st_[:, 0:ncols], src[c * 128:(c + 1) * 128, col0:col0 + ncols], [], [st_])
                ts("dve", w[:, c, dst0:dst0 + ncols], st_[:, 0:ncols], gain[:, c:c + 1], None, ALU.mult, None, [st_, gain], [w])

        def front(x_t, hbt, ssq_, rstd_, tbank, hTt, extra_scale=None, tn=None):
            act(hbt[:], x_t[:], AF.Square, [x_t], [hbt, ssq_], accum_out=ssq_[:, 0:1])
            if tn is not None:
                rstd_dve(ssq_, 1, D, rstd_, tn)
            else:
                rstd_from_ssq(ssq_, 1, D, rstd_)
            if extra_scale is not None:
                tt("dve", rstd_[:, 0:1], rstd_[:, 0:1], extra_scale, ALU.mult, [rstd_, hval], [rstd_])
            ts("dve", hbt[:], x_t[:], rstd_[:, 0:1], None, ALU.mult, None, [x_t, rstd_], [hbt])
            transposes([hbt[:, c * 128:(c + 1) * 128] for c in range(8)], 128, tbank, hTt[:], hTt, [hbt])

        def proj(pbank, col0, ncols, lhs, wmat):
            for c in range(8):
                mm(PB[pbank][:, 0:ncols], lhs[:, c, :], wmat[:, c, col0:col0 + ncols], c == 0, c == 7, [lhs, wmat], [PB[pbank]])

        def norm_rep(src_t, src_ap_o, src_ap_d, n, dst_ap, dst_t, rr):
            recip(rr[0:64, 0:n], src_ap_d, [src_t], [rr])
            tt("dve", dst_ap, src_ap_o, rr[0:64, 0:n], ALU.mult, [src_t, rr], [dst_t])

        with contextlib.ExitStack() as s1:
            g_mix = sbuf(s1, "g_mix", [128, 8], F32)
            g_mem = sbuf(s1, "g_mem", [128, 8], F32)
            qan = sbuf(s1, "qan", [128, 3], F32)
            kvan = sbuf(s1, "kvan", [128, 2], F32)
            for t_, d_ in ((g_mix, g_mix_d), (g_mem, g_mem_d), (qan, qan_d), (kvan, kvan_d)):
                dma("sp", t_[:], d_, [], [t_])
            xt = [sbuf(s1, f"xt{i}", [128, D], F32) for i in range(2)]
            ssq1 = sbuf(s1, "ssq1", [128, 8], F32)
            rstd1 = sbuf(s1, "rstd1", [128, 8], F32)
            ssqL = sbuf(s1, "ssqL", [128, 8], F32)
            rstdL = sbuf(s1, "rstdL", [128, 8], F32)
            hb = sbuf(s1, "hb", [128, D], BF16)
            hT = [sbuf(s1, f"hT{i}", [128, 8, 128], BF16) for i in range(2)]
            nb = sbuf(s1, "nb", [128, 384], BF16)
            nbT = sbuf(s1, "nbT", [128, 3, 128], BF16)
            ropet = [sbuf(s1, f"ropet{i}", [128, 96], F32) for i in range(3)]
            tmpA = sbuf(s1, "tmpA", [128, 8, 96], F32)
            sqA = sbuf(s1, "sqA", [128, 8, 128], F32)
            ssqA = sbuf(s1, "ssqA", [128, 8], F32)
            rstdA = sbuf(s1, "rstdA", [128, 8], F32)
            tmpS = sbuf(s1, "tmpS", [128, 8, 64], F32)
            sqS = sbuf(s1, "sqS", [128, 8, 128], F32)
            ssqS = sbuf(s1, "ssqS", [128, 8], F32)
            rstdS = sbuf(s1, "rstdS", [128, 8], F32)
            kq = sbuf(s1, "kq", [128, 8, 96], BF16)
            kqT = [sbuf(s1, f"kqT{i}", [96, 8, 128], BF16) for i in range(2)]
            ks = sbuf(s1, "ks", [128, 8, 64], BF16)
            ksT = [sbuf(s1, f"ksT{i}", [64, 8 * 128], BF16) for i in range(2)]
            KmT = [sbuf(s1, f"KmT{j}", [128, 4, 256], BF16) for j in range(2)]
            Vm = [sbuf(s1, f"Vm{j}", [128, 2, 512], BF16) for j in range(2)]

            def lora(pbank_in, ncol, nch, w2, outbanks, ncols_out, tbank):
                src = PB[pbank_in][:, 0:ncol]
                act(nb[:, 0:ncol], src, AF.Square, [PB[pbank_in]], [nb, ssqL], accum_out=ssqL[:, 0:1])
                rstd_from_ssq(ssqL, 1, ncol, rstdL)
                ts("dve", nb[:, 0:ncol], src, rstdL[:, 0:1], None, ALU.mult, None, [PB[pbank_in], rstdL], [nb])
                transposes([nb[:, c * 128:(c + 1) * 128] for c in range(nch)], 128, tbank, nbT[:, 0:nch, :], nbT, [nb])
                for oi, ob in enumerate(outbanks):
                    for c in range(nch):
                        mm(PB[ob][:, 0:ncols_out], nbT[:, c, :], w2[:, c, oi * ncols_out:(oi + 1) * ncols_out],
                           c == 0, c == nch - 1, [nbT, w2], [PB[ob]])

            with contextlib.ExitStack() as sA:
                w_inA = sbuf(sA, "w_inA", [128, 8, 544], BF16)
                w_kvb = sbuf(sA, "w_kvb", [128, 2, 1024], BF16)
                with contextlib.ExitStack() as sL:
                    stageA = [sbuf(sL, "stageA%d" % i_, [128, 1024], F32) for i_ in range(2)]
                    load_cols(w_inA, 0, w_in_d, 384, 288, 8, stageA, g_mix)
                    load_cols(w_inA, 288, w_in_d, 1184, 256, 8, stageA, g_mix)
                    load_weight(w_kvb, w_kvb_d, 2, 1024, stageA, gain=kvan)
                P.barrier()
                kcat = [sbuf(sA, f"kcat{i}", [128, 8, 96], F32) for i in range(2)]
                zks = [sbuf(sA, f"zks{i}", [128, 2, 64], F32) for i in range(2)]
                vsb = [sbuf(sA, f"vsb{i}", [128, 8, 128], BF16) for i in range(2)]
                vss = [sbuf(sA, f"vss{i}", [128, 2, 128], BF16) for i in range(2)]
                zeros = sbuf(sA, "zeros", [128, 512], BF16)
                memset("dve", zeros[:], 0.0, [zeros])
                for i in range(2):
                    memset("dve", vsb[i][:], 1.0, [vsb[i]])
                    memset("dve", vss[i][:], 1.0, [vss[i]])
                for j in range(2):
                    KB = jobs[j]["KB"]
                    for blk in (0, KB + 1):
                        dma("pool", KsT_d[j][:, :, blk * 128:(blk + 1) * 128].rearrange("h d t -> d h t"),
                            zeros[0:64, 0:256].rearrange("p (h t) -> p h t", h=2), [zeros], [])
                        dma("pool", Vs_d[j][blk], zeros[:, 0:256], [zeros], [])
                    ncol = jobs[j]["NQ"] * 128 + 2
                    for col in (0, ncol - 1):
                        dma("pool", H2_d[j][:, :, col:col + 1].rearrange("c p o -> p c o"),
                            zeros[:, 0:8].rearrange("p (c o) -> p c o", o=1), [zeros], [], slow=True)
                blocksA = [(j, kb) for j in range(2) for kb in range(jobs[j]["KB"])]

                def A0(bi):
                    j, kb = blocksA[bi]
                    x_t = xt[bi % 2]
                    dma("sp", x_t[:], xkv[j][kb * 128:(kb + 1) * 128, :], [], [x_t])
                    dma("sp", ropet[bi % 3][:], rope_d[j][kb], [], [ropet[bi % 3]])
                    front(x_t, hb, ssq1, rstd1, 0, hT[bi % 2])
                    q_ = kb - jobs[j]["qoff"]
                    if 0 <= q_ < jobs[j]["NQ"]:
                        dma("pool", HT_d[j][:, :, q_ * 128:(q_ + 1) * 128].rearrange("c p t -> p c t"), hT[bi % 2][:], [hT[bi % 2]], [])

                swa_last = [jobs[0]["KB"], jobs[1]["qoff"] + jobs[1]["NQ"] + 1]

                def A1(bi):
                    j, kb = blocksA[bi]
                    xi = bi % 2
                    proj(1, 0, 288, hT[xi], w_inA)
                    if kb <= swa_last[j]:
                        proj(2, 288, 256, hT[xi], w_inA)
                    lora(1, 256, 2, w_kvb, [3, 4], 512, 5)
                    cp("dve", kcat[xi][:, :, 64:96], PB[1][:, 256:288].unsqueeze(1).to_broadcast([128, 8, 32]), [PB[1]], [kcat[xi]])
                    if kb <= swa_last[j]:
                        cp("act", zks[xi][:], PB[2][:, 0:128].rearrange("p (h d) -> p h d", h=2), [PB[2]], [zks[xi]])
                        cp("act", vss[xi][:, :, 0:64], PB[2][:, 128:256].rearrange("p (h d) -> p h d", h=2), [PB[2]], [vss[xi]])
                    for half in range(2):
                        kv3 = PB[3 + half][:].rearrange("p (h d) -> p h d", h=4)
                        cp("dve", kcat[xi][:, half * 4:(half + 1) * 4, 0:64], kv3[:, :, 0:64], [PB[3 + half]], [kcat[xi]])
                        cp("act", vsb[xi][:, half * 4:(half + 1) * 4, 0:64], kv3[:, :, 64:128], [PB[3 + half]], [vsb[xi]])

                def A2a(bi):
                    j, kb = blocksA[bi]
                    xi = bi % 2
                    rt = ropet[bi % 3]
                    dma("pool", V_d[j][:, :, kb, :].rearrange("h p e -> p h e"), vsb[xi][:], [vsb[xi]], [])
                    headnorm(kcat[xi][:], [kcat[xi]], 8, 96, gt["gkm"], kq[:], kq, tmpA, sqA, ssqA, rstdA, rope=(rt[:, 0:32], 16, rt))
                    transposes([kq[:, h, :] for h in range(8)], 96, 6, kqT[xi][:], kqT[xi], [kq], evac="act")
                    dma("pool", KT_d[j][:, :, kb * 128:(kb + 1) * 128].rearrange("h d t -> d h t"), kqT[xi][:], [kqT[xi]], [])

                def A2b(bi):
                    j, kb = blocksA[bi]
                    if kb > swa_last[j]:
                        return
                    xi = bi % 2
                    rt = ropet[bi % 3]
                    dma("pool", Vs_d[j][kb + 1], vss[xi][:].rearrange("p h e -> p (h e)"), [vss[xi]], [])
                    headnorm(zks[xi][:], [zks[xi]], 2, 64, gt["gks"], ks[:, 0:2, :], ks, tmpS, sqS, ssqS, rstdS, rope=(rt[:, 32:96], 32, rt))
                    kv_ = ksT[xi][:, 0:256].rearrange("p (h t) -> p h t", h=2)
                    transposes([ks[:, h, :] for h in range(2)], 64, 7, kv_, ksT[xi], [ks], evac="act")
                    dma("pool", KsT_d[j][:, :, (kb + 1) * 128:(kb + 2) * 128].rearrange("h d t -> d h t"), kv_, [ksT[xi]], [])

                pipeline(len(blocksA), [[A0], [A1], [A2a, A2b]])
            P.barrier()
            with contextlib.ExitStack() as sM:
                stageM = sbuf(sM, "stageM", [128, 1024], F32)
                w_memkv = sbuf(sM, "w_memkv", [128, 8, 1024], BF16)
                load_weight(w_memkv, w_memkv_d, 8, 1024, stageM, gain=g_mem)
                kmn = sbuf(sM, "kmn", [128, 4, 128], BF16)
                sqm = sbuf(sM, "sqm", [128, 4, 128], F32)
                tmpm = sbuf(sM, "tmpm", [128, 4, 128], F32)
                for j in range(2):
                    for mb in range(2):
                        x_t = xt[mb]
                        dma("sp", x_t[:], mem[j][mb * 128:(mb + 1) * 128, :], [], [x_t])
                        front(x_t, hb, ssq1, rstd1, 0, hT[0])
                        proj(1, 0, 512, hT[0], w_memkv)
                        proj(2, 512, 512, hT[0], w_memkv)
                        headnorm(PB[1][:].rearrange("p (h d) -> p h d", h=4), [PB[1]], 4, 128, gt["gke"], kmn[:], kmn,
                                 tmpm, sqm, ssqA, rstdA)
                        transposes([kmn[:, h, :] for h in range(4)], 128, 0, KmT[j][:, :, mb * 128:(mb + 1) * 128], KmT[j], [kmn])
                        cp("act", Vm[j][:, mb, :], PB[2][:], [PB[2]], [Vm[j]])
            P.barrier()
            with contextlib.ExitStack() as sB:
                w_inB = sbuf(sB, "w_inB", [128, 8, 1408], BF16)
                w_qb = sbuf(sB, "w_qb", [128, 3, 768], BF16)
                with contextlib.ExitStack() as sL:
                    stageB = [sbuf(sL, "stageB%d" % i_, [128, 1024], F32) for i_ in range(2)]
                    load_cols(w_inB, 0, w_in_d, 0, 384, 8, stageB, g_mix)
                    load_cols(w_inB, 384, w_in_d, 672, 512, 8, stageB, g_mix)
                    load_cols(w_inB, 896, w_in_d, 1440, 512, 8, stageB, g_mix)
                    load_weight(w_qb, w_qb_d, 3, 768, stageB, gain=qan)
                P.barrier()
                zq = [sbuf(sB, f"zq{i}", [128, 8, 96], F32) for i in range(2)]
                zs = [sbuf(sB, f"zs{i}", [128, 8, 64], F32) for i in range(2)]
                zm = [sbuf(sB, f"zm{i}", [128, 4, 128], F32) for i in range(2)]
                sqM = sbuf(sB, "sqM", [128, 4, 128], F32)
                ssqM = sbuf(sB, "ssqM", [128, 8], F32)
                rstdM = sbuf(sB, "rstdM", [128, 8], F32)
                qm = sbuf(sB, "qm", [128, 4, 128], BF16)
                qmT = [sbuf(sB, f"qmT{i}", [128, 4, 128], BF16) for i in range(2)]
                kst = [sbuf(sB, f"kst{i}", [64, 2, 384], BF16) for i in range(2)]
                vst = [sbuf(sB, f"vst{i}", [128, 3, 256], BF16) for i in range(2)]
                pt = [sbuf(sB, f"pt{i}", [128, 512], BF16) for i in range(3)]
                rrS = sbuf(sB, "rrS", [64, 512], F32)
                osT = [sbuf(sB, f"osT{i}", [64, 8 * 128], BF16) for i in range(2)]
                pm = sbuf(sB, "pm", [128, 256], BF16)
                pmT = sbuf(sB, "pmT", [128, 2, 128], BF16)
                rsum = sbuf(sB, "rsum", [128, 4], F32)
                rrec = sbuf(sB, "rrec", [128, 4], F32)
                om = sbuf(sB, "om", [128, 4, 128], BF16)
                omT = [sbuf(sB, f"omT{i}", [128, 4, 128], BF16) for i in range(2)]
                blocksB = [(j, q) for j in range(2) for q in range(jobs[j]["NQ"])]

                def B01(bi):
                    j, q = blocksB[bi]
                    kblk = jobs[j]["qoff"] + q
                    xi = bi % 2
                    dma("sp", ropet[xi][:], rope_d[j][kblk], [], [ropet[xi]])
                    hTt = hT[xi]
                    dma("sp", hTt[:], HT_d[j][:, :, q * 128:(q + 1) * 128].rearrange("c p t -> p c t"), [], [hTt])
                    proj(1, 0, 384, hTt, w_inB)
                    lora(1, 384, 3, w_qb, [0, 1], 384, 0)
                    for half, bk in ((0, 0), (1, 1)):
                        cp("act" if half else "dve", zq[xi][:, half * 4:(half + 1) * 4, :],
                           PB[bk][:, 0:384].rearrange("p (h d) -> p h d", h=4), [PB[bk]], [zq[xi]])
                    proj(0, 384, 512, hTt, w_inB)
                    cp("act", zs[xi][:], PB[0][:].rearrange("p (h d) -> p h d", h=8), [PB[0]], [zs[xi]])
                    proj(1, 896, 512, hTt, w_inB)
                    cp("dve", zm[xi][:], PB[1][:].rearrange("p (h d) -> p h d", h=4), [PB[1]], [zm[xi]])

                def B2a(bi):
                    j, q = blocksB[bi]
                    xi = bi % 2
                    rt = ropet[xi]
                    headnorm(zq[xi][:], [zq[xi]], 8, 96, gt["gqm"], kq[:], kq, tmpA, sqA, ssqA, rstdA, rope=(rt[:, 0:32], 16, rt))
                    transposes([kq[:, h, :] for h in range(8)], 96, 2, kqT[xi][:], kqT[xi], [kq], evac="act")
                    dma("pool", QT_d[j][:, :, q * 128:(q + 1) * 128].rearrange("h d t -> d h t"), kqT[xi][:], [kqT[xi]], [])

                def B2bc(bi):
                    j, q = blocksB[bi]
                    xi = bi % 2
                    rt = ropet[xi]
                    kblk = jobs[j]["qoff"] + q
                    dma("sp", kst[xi][:], KsT_d[j][:, :, kblk * 128:(kblk + 3) * 128].rearrange("h d t -> d h t"), [], [kst[xi]])
                    dma("sp", vst[xi][:], Vs_d[j][kblk:kblk + 3].rearrange("b p e -> p b e"), [], [vst[xi]])
                    headnorm(zs[xi][:], [zs[xi]], 8, 64, gt["gqs"], ks[:], ks, tmpS, sqS, ssqS, rstdS, rope=(rt[:, 32:96], 32, rt))
                    transposes([ks[:, h, :] for h in range(8)], 64, 3, ksT[xi][:].rearrange("p (h t) -> p h t", h=8), ksT[xi], [ks], evac="act")
                    z = zm[xi]
                    sv = sqM[:]
                    act(sv, z[:], AF.Square, [z], [sqM])
                    red(ssqM[:, 0:4], sv, [sqM], [ssqM])
                    rstd_from_ssq(ssqM, 4, 128, rstdM)
                    tt("dve", sv, z[:], rstdM[:, 0:4].unsqueeze(2).to_broadcast([128, 4, 128]), ALU.mult, [z, rstdM], [sqM])
                    tt("dve", qm[:], sv, gt["gqe"][:, 0:128].unsqueeze(1).to_broadcast([128, 4, 128]), ALU.mult, [sqM, gt["gqe"]], [qm])
                    transposes([qm[:, h, :] for h in range(4)], 128, 3, qmT[xi][:], qmT[xi], [qm], evac="act")

                def B3a(bi):
                    j, q = blocksB[bi]
                    xi = bi % 2
                    qg = jobs[j]["qg0"] + q
                    ksTt = ksT[xi]
                    ost = osT[xi]
                    for hk in range(2):
                        for jj in range(3):
                            sbk = 4
                            mm(PB[sbk][:], kst[xi][:, hk, jj * 128:(jj + 1) * 128], ksTt[:, hk * 512:(hk + 1) * 512], True, True,
                               [kst[xi], ksTt], [PB[sbk]])
                            act(pt[jj][:], PB[sbk][:], AF.Exp, [PB[sbk], swab], [pt[jj]], scale=0.125,
                                bias=swab[:, qg * 3 + jj:qg * 3 + jj + 1])
                            if jj != 1:
                                mi = 0 if jj == 0 else 1
                                p3 = pt[jj][:].rearrange("p (g t) -> p g t", g=4)
                                tt("dve", p3, p3, bandm[:, mi, :].unsqueeze(1).to_broadcast([128, 4, 128]), ALU.mult,
                                   [pt[jj], bandm], [pt[jj]])
                        for jj in range(3):
                            mm(PB[5][:], vst[xi][:, jj, hk * 128:(hk + 1) * 128], pt[jj][:], jj == 0, False, [vst[xi], pt[jj]], [PB[5]])
                        mm(PB[5][:], e128[0:1, :], sinkrow[0:1, hk * 512:(hk + 1) * 512], False, True, [e128, sinkrow], [PB[5]])
                        act(rrS[:], PB[5][64:128, :], AF.Ln, [PB[5]], [rrS])
                        act(rrS[:], rrS[:], AF.Exp, [rrS], [rrS], scale=-1.0)
                        tt("dve", ost[:, hk * 512:(hk + 1) * 512], PB[5][0:64, :], rrS[:], ALU.mult, [PB[5], rrS], [ost])
                    dma("pool", OS_d[j][:, :, q * 128:(q + 1) * 128].rearrange("c (two d) t -> d (c two) t", two=2),
                        ost[:].rearrange("p (h t) -> p h t", h=8), [ost], [])

                def B3b(bi):
                    j, q = blocksB[bi]
                    xi = bi % 2
                    for h in range(4):
                        mm(PB[6][:, 0:256], qmT[xi][:, h, :], KmT[j][:, h, :], True, True, [qmT[xi], KmT[j]], [PB[6]])
                        act(pm[:], PB[6][:, 0:256], AF.Exp, [PB[6]], [pm, rsum], scale=128 ** -0.5, accum_out=rsum[:, h:h + 1])
                        transposes([pm[:, c * 128:(c + 1) * 128] for c in range(2)], 128, 6, pmT[:], pmT, [pm], slot0=4)
                        for c in range(2):
                            mm(PB[7][:, h * 128:(h + 1) * 128], pmT[:, c, :], Vm[j][:, c, h * 128:(h + 1) * 128], c == 0, c == 1,
                               [pmT, Vm[j]], [PB[7]])
                    recip(rrec[:], rsum[:], [rsum], [rrec])
                    tt("dve", om[:], PB[7][:].rearrange("p (h d) -> p h d", h=4), rrec[:].unsqueeze(2).to_broadcast([128, 4, 128]),
                       ALU.mult, [PB[7], rrec], [om])
                    transposes([om[:, h, :] for h in range(4)], 128, 6, omT[xi][:], omT[xi], [om], slot0=4)
                    dma("pool", OM_d[j][:, :, q * 128:(q + 1) * 128].rearrange("c p t -> p c t"), omT[xi][:], [omT[xi]], [])

                pipeline(len(blocksB), [[B01], [B2a, B2bc], [B3a, B3b]])
        P.barrier()
        with contextlib.ExitStack() as s2:
            SMAX = max(j["S"] for j in jobs)
            KBMAX = SMAX // 128
            KTh = [sbuf(s2, f"KTh{i}", [96, SMAX], BF16) for i in range(2)]
            Vh = [sbuf(s2, f"Vh{i}", [128, KBMAX, 128], BF16) for i in range(2)]
            QTh = [sbuf(s2, f"QTh{i}", [96, 2048], BF16) for i in range(2)]
            ptc = [sbuf(s2, f"ptc{i}", [128, 512], BF16) for i in range(4)]
            otsb = [sbuf(s2, f"otsb{i}", [64, 512], BF16) for i in range(2)]
            otc = [sbuf(s2, f"otc{i}", [128, 512], F32) for i in range(4)]
            rrC = sbuf(s2, "rrC", [64, 512], F32)
            sc_mla = 96 ** -0.5
            it = 0
            oc = 0
            for j in range(2):
                S, KB, NQ = jobs[j]["S"], jobs[j]["KB"], jobs[j]["NQ"]
                ntok = NQ * 128
                for g0 in range(0, ntok, 2048):
                    G = min(2048, ntok - g0)
                    qtiles = [(o, min(512, G - o)) for o in range(0, G, 512)]
                    for h in range(8):
                        bi = it % 2
                        it += 1
                        dma("sp", KTh[bi][:, 0:S], KT_d[j][h], [], [KTh[bi]])
                        dma("sp", Vh[bi][:, 0:KB, :], V_d[j][h], [], [Vh[bi]])
                        dma("sp", QTh[bi][:, 0:G], QT_d[j][h][:, g0:g0 + G], [], [QTh[bi]])
                        steps = [(kb, ti) for kb in range(KB) for ti in range(len(qtiles))]
                        ns = len(steps)
                        for i in range(ns + 2):
                            if i < ns:
                                kb, ti = steps[i]
                                o, n = qtiles[ti]
                                sbk = 4 + i % 3
                                mm(PB[sbk][:, 0:n], KTh[bi][:, kb * 128:(kb + 1) * 128], QTh[bi][:, o:o + n], True, True,
                                   [KTh[bi], QTh[bi]], [PB[sbk]])
                                act(ptc[i % 4][:, 0:n], PB[sbk][:, 0:n], AF.Exp, [PB[sbk]], [ptc[i % 4]], scale=sc_mla)
                            if i >= 2:
                                kb, ti = steps[i - 2]
                                o, n = qtiles[ti]
                                mm(PB[ti][:, 0:n], Vh[bi][:, kb, :], ptc[(i - 2) % 4][:, 0:n], kb == 0, kb == KB - 1,
                                   [Vh[bi], ptc[(i - 2) % 4]], [PB[ti]])
                        for ti, (o, n) in enumerate(qtiles):
                            cp("dve", otc[ti][:, 0:n], PB[ti][:, 0:n], [PB[ti]], [otc[ti]])
                        for ti, (o, n) in enumerate(qtiles):
                            ob = otsb[oc % 2]
                            oc += 1
                            norm_rep(otc[ti], otc[ti][0:64, 0:n], otc[ti][64:128, 0:n], n, ob[:, 0:n], ob, rrC)
                            dma("pool", OT_d[j][h // 2, (h % 2) * 64:(h % 2) * 64 + 64, g0 + o:g0 + o + n], ob[:, 0:n], [ob], [])
        P.barrier()
        with contextlib.ExitStack() as s3:
            g_mixD = sbuf(s3, "g_mixD", [128, 8], F32)
            dma("sp", g_mixD[:], g_mix_d, [], [g_mixD])
            w_g = sbuf(s3, "w_g", [128, 8, 3 * D], BF16)
            w_o3 = [sbuf(s3, f"w_o3_{b_}", [128, 4, D], BF16) for b_ in range(3)]
            w_out = sbuf(s3, "w_out", [128, 8, D], BF16)
            with contextlib.ExitStack() as sL:
                stageD = [sbuf(sL, "stageD%d" % i_, [128, 3 * D], F32) for i_ in range(2)]
                load_cols(w_g, 0, w_in_d, 1952, 3 * D, 8, stageD, g_mixD)
                for b_, wd in enumerate((w_omla_d, w_oswa_d, w_omem_d)):
                    load_weight(w_o3[b_], wd, 4, D, stageD)
                load_weight(w_out, w_out_d, 8, D, stageD)
            P.barrier()
            xtd = [sbuf(s3, f"xtd{i}", [128, D], F32) for i in range(3)]
            hbD = sbuf(s3, "hbD", [128, D], BF16)
            ssq0 = sbuf(s3, "ssq0", [128, 8], F32)
            rstd0 = sbuf(s3, "rstd0", [128, 8], F32)
            tn0 = sbuf(s3, "tn0", [128, 8], F32)
            tnD = sbuf(s3, "tnD", [128, 8], F32)
            hTD = [sbuf(s3, f"hTD{i}", [128, 8, 128], BF16) for i in range(2)]
            o3 = [[sbuf(s3, f"o3_{b_}_{i}", [128, 4, 128], BF16) for i in range(2)] for b_ in range(3)]
            gbufD = [sbuf(s3, f"gbufD{i}", [128, 512], F32) for i in range(2)]
            maccD = [sbuf(s3, f"maccD{i}", [128, 512], F32) for i in range(2)]
            tbD = [sbuf(s3, f"tbD{i}", [128, 512], F32) for i in range(2)]
            merged = [sbuf(s3, f"merged{i}", [128, D], BF16) for i in range(2)]
            mT = sbuf(s3, "mT", [128, 8, 128], BF16)
            xn = [sbuf(s3, f"xn{i}", [128, D], F32) for i in range(2)]
            ssqD = sbuf(s3, "ssqD", [128, 8], F32)
            rstdD = sbuf(s3, "rstdD", [128, 8], F32)
            h2b = sbuf(s3, "h2b", [128, D], BF16)
            h2T = [sbuf(s3, f"h2T{i}", [128, 8, 128], BF16) for i in range(2)]
            blocksD = [(j, q) for j in range(2) for q in range(jobs[j]["NQ"])]
            o3_d = (OT_d, OS_d, OM_d)

            def D0(bi):
                j, q = blocksD[bi]
                qoff = jobs[j]["qoff"]
                x_t = xtd[bi % 3]
                dma("sp", x_t[:], xkv[j][(qoff + q) * 128:(qoff + q + 1) * 128, :], [], [x_t])
                for b_ in range(3):
                    dma("sp", o3[b_][bi % 2][:], o3_d[b_][j][:, :, q * 128:(q + 1) * 128].rearrange("c p t -> p c t"), [], [o3[b_][bi % 2]])
                dma("sp", hTD[bi % 2][:], HT_d[j][:, :, q * 128:(q + 1) * 128].rearrange("c p t -> p c t"), [], [hTD[bi % 2]])

            def mk_Dm(nh):
                gb, pb_ = 1 + 2 * nh, 2 + 2 * nh

                def Dm(bi):
                    xi = bi % 2
                    for b_ in range(3):
                        proj(gb, b_ * D + nh * 512, 512, hTD[xi], w_g)
                        act(gbufD[nh][:], PB[gb][:], AF.Sigmoid, [PB[gb]], [gbufD[nh]])
                        for c in range(4):
                            mm(PB[pb_][:], o3[b_][xi][:, c, :], w_o3[b_][:, c, nh * 512:(nh + 1) * 512], c == 0, c == 3,
                               [o3[b_][xi], w_o3[b_]], [PB[pb_]])
                        if b_ == 0:
                            tt("dve", maccD[nh][:], PB[pb_][:], gbufD[nh][:], ALU.mult, [PB[pb_], gbufD[nh]], [maccD[nh]])
                        else:
                            tt("dve", tbD[nh][:], PB[pb_][:], gbufD[nh][:], ALU.mult, [PB[pb_], gbufD[nh]], [tbD[nh]])
                            if b_ == 1:
                                tt("pool", maccD[nh][:], maccD[nh][:], tbD[nh][:], ALU.add, [maccD[nh], tbD[nh]], [maccD[nh]])
                            else:
                                tt("pool", merged[xi][:, nh * 512:(nh + 1) * 512], maccD[nh][:], tbD[nh][:], ALU.add,
                                   [maccD[nh], tbD[nh]], [merged[xi]])
                return Dm

            def D2(bi):
                j, q = blocksD[bi]
                xi = bi % 2
                qg = jobs[j]["qg0"] + q
                x_t = xtd[bi % 3]
                transposes([merged[xi][:, c * 128:(c + 1) * 128] for c in range(8)], 128, 5, mT[:], mT, [merged[xi]], evac="act")
                for nh in range(2):
                    for c in range(8):
                        mm(PB[6 + nh][:], mT[:, c, :], w_out[:, c, nh * 512:(nh + 1) * 512], c == 0, c == 7, [mT, w_out], [PB[6 + nh]])
                    tt("dve", xn[xi][:, nh * 512:(nh + 1) * 512], PB[6 + nh][:], x_t[:, nh * 512:(nh + 1) * 512], ALU.add,
                       [PB[6 + nh], x_t], [xn[xi]])
                dma("pool", XN_d[j][q * 128:(q + 1) * 128, :], xn[xi][:], [xn[xi]], [])
                front(xn[xi], h2b, ssqD, rstdD, 5, h2T[xi], extra_scale=hval[:, qg:qg + 1], tn=tnD)
                dma("pool", H2_d[j][:, :, 1 + q * 128:1 + (q + 1) * 128].rearrange("c p t -> p c t"), h2T[xi][:], [h2T[xi]], [])

            pipeline(len(blocksD), [[D0], [mk_Dm(0), mk_Dm(1)], [D2]])
        P.barrier()
        with contextlib.ExitStack() as s4:
            g_ffn = sbuf(s4, "g_ffn", [128, 8], F32)
            convw = sbuf(s4, "convw", [128, NCH_FF, 3], F32)
            convb = sbuf(s4, "convb", [128, NCH_FF], F32)
            dma("sp", g_ffn[:], g_ffn_d, [], [g_ffn])
            dma("sp", convw[:], convw_d, [], [convw])
            dma("sp", convb[:], convb_d, [], [convb])
            stageE = [sbuf(s4, "stageE%d" % i_, [128, 704], F32) for i_ in range(2)]
            w_up = sbuf(s4, "w_up", [128, 8, 2 * D_FF], BF16)
            w_down = sbuf(s4, "w_down", [128, 22, D], BF16)
            load_weight(w_up, w_up_d, 8, 2 * D_FF, stageE, gain=g_ffn, cpiece=704)
            load_weight(w_down, w_down_d, 22, D, stageE, cpiece=512)
            h2t = sbuf(s4, "h2t", [128, 8, FT + 2], BF16)
            xnt = [sbuf(s4, f"xnt{i}", [128, D], F32) for i in range(2)]
            gT = sbuf(s4, "gT", [128, 22, FT], BF16)
            ub = [sbuf(s4, f"ub{i}", [128, FT + 2], F32) for i in range(2)]
            cc = [sbuf(s4, f"cc{i}", [128, FT], F32) for i in range(4)]
            yo = sbuf(s4, "yo", [128, D], F32)
            ucnt = 0
            ycnt = 0
            STEP = FT - 2
            for j in range(2):
                f0, fn = jobs[j]["f0"], jobs[j]["fn"]
                T0, NT = f0 * 128, fn * 128
                for c0 in range(0, NT, STEP):
                    n = min(STEP, NT - c0)
                    tok0 = T0 + c0
                    dma("sp", h2t[:, :, 0:n + 2], H2_d[j][:, :, tok0:tok0 + n + 2].rearrange("c p t -> p c t"), [], [h2t])
                    for i2 in range(22):
                        for which in range(2):
                            ch = i2 + 22 * which
                            u = ub[ucnt % 2]
                            bm = 1 + (ucnt % 4)
                            ucnt += 1
                            for k in range(8):
                                mm(PB[bm][:, 0:n + 2], w_up[:, k, ch * 128:(ch + 1) * 128], h2t[:, k, 0:n + 2], k == 0, k == 7, [w_up, h2t], [PB[bm]])
                            cp("act", u[:, 0:n + 2], PB[bm][:, 0:n + 2], [PB[bm]], [u])
                            c_ = cc[2 * (i2 % 2) + which]
                            ts("dve", c_[:, 0:n], u[:, 1:n + 1], convw[:, ch, 1:2], convb[:, ch:ch + 1], ALU.mult, ALU.add, [u, convw, convb], [c_])
                            stt("dve", c_[:, 0:n], u[:, 0:n], convw[:, ch, 0:1], c_[:, 0:n], ALU.mult, ALU.add, [u, convw, c_], [c_])
                            stt("dve", c_[:, 0:n], u[:, 2:n + 2], convw[:, ch, 2:3], c_[:, 0:n], ALU.mult, ALU.add, [u, convw, c_], [c_])
                        ca, cv = cc[2 * (i2 % 2)], cc[2 * (i2 % 2) + 1]
                        act(ca[:, 0:n], ca[:, 0:n], AF.Silu, [ca], [ca])
                        tt("pool", gT[:, i2, 0:n], ca[:, 0:n], cv[:, 0:n], ALU.mult, [ca, cv], [gT])
                    for tb0 in range(0, n, 128):
                        m = min(128, n - tb0)
                        xi = ycnt % 2
                        ycnt += 1
                        row0 = tok0 + tb0
                        dma("sp", xnt[xi][0:m, :], XN_d[j][row0:row0 + m, :], [], [xnt[xi]])
                        for nh in range(2):
                            for i2 in range(22):
                                mm(PB[5 + nh][0:m, :], gT[:, i2, tb0:tb0 + m], w_down[:, i2, nh * 512:(nh + 1) * 512], i2 == 0, i2 == 21,
                                   [gT, w_down], [PB[5 + nh]])
                            tt("dve", yo[0:m, nh * 512:(nh + 1) * 512], PB[5 + nh][0:m, :], xnt[xi][0:m, nh * 512:(nh + 1) * 512], ALU.add,
                               [PB[5 + nh], xnt[xi]], [yo])
                        orow = row0 - T0
                        dma("pool", y_d[j][orow:orow + m, :], yo[0:m, :], [yo], [])
        P.emit()
    return nc


ROPE_THETA = 10000.0


def _rope_tab(pos, dim):
    inv = (1.0 / (ROPE_THETA ** (np.arange(0, dim, 2, dtype=np.float32) / np.float32(dim)))).astype(np.float32)
    ang = pos.astype(np.float32)[:, None] * inv[None, :]
    return np.cos(ang).astype(np.float32), np.sin(ang).astype(np.float32)


def make_core_inputs(c, inp, SP, SS):
    QS = SS // 4
    f32 = np.float32
    s = c // 4
    t0 = QS * (c % 4)
    shift = t0 - 256
    idx1 = (np.arange(SS) + shift) % SS
    pos = [np.arange(SP), idx1]
    m = {}
    m["xkv0"] = np.ascontiguousarray(inp["x_prompt"][c], dtype=f32)
    m["xkv1"] = np.ascontiguousarray(np.asarray(inp["x_sample"][s])[idx1], dtype=f32)
    m["mem0"] = np.ascontiguousarray(inp["mem_prompt"][c], dtype=f32)
    m["mem1"] = np.ascontiguousarray(inp["mem_sample"][s], dtype=f32)
    for j in range(2):
        cm, sm = _rope_tab(pos[j], 32)
        cs, ss_ = _rope_tab(pos[j], 64)
        tab = np.concatenate([cm, sm, cs, ss_], axis=1)
        m[f"rope{j}"] = np.ascontiguousarray(tab.reshape(-1, 128, 96))
    NQ0, NQ1 = SP // 128, QS // 128 + 2
    hv = np.ones((NQ0 + NQ1,), f32)
    sw = np.zeros((NQ0 + NQ1, 3), f32)
    sw[0, 0] = NEG
    sw[NQ0 - 1, 2] = NEG
    nblk = SS // 128
    for q in range(NQ1):
        p = (t0 // 128) - 1 + q
        if p - 1 < 0 or p - 1 > nblk - 1:
            sw[NQ0 + q, 0] = NEG
        if p + 1 > nblk - 1 or p + 1 < 0:
            sw[NQ0 + q, 2] = NEG
        if p < 0 or p > nblk - 1:
            hv[NQ0 + q] = 0.0
    m["hval"] = np.ascontiguousarray(np.broadcast_to(hv[None, :], (128, NQ0 + NQ1)))
    m["swab"] = np.ascontiguousarray(np.broadcast_to(sw.reshape(1, -1), (128, (NQ0 + NQ1) * 3)))
    k = np.arange(128)[:, None]
    qq = np.arange(128)[None, :]
    band = np.stack([(k >= qq), (k <= qq)], axis=1).astype(f32)
    m["bandm"] = band.astype(ml_dtypes.bfloat16)
    m["ident"] = np.eye(128, dtype=f32).astype(ml_dtypes.bfloat16)

    def colvec(v, n):
        return np.ascontiguousarray(np.asarray(v, f32).reshape(n, 128).T)

    m["w_in"] = np.ascontiguousarray(inp["w_in"][0], dtype=f32)
    m["g_mix"] = colvec(inp["g_mix"][0], 8)
    m["g_mem"] = colvec(inp["g_mem"][0], 8)
    m["g_ffn"] = colvec(inp["g_ffn"][0], 8)
    m["q_a_norm"] = colvec(inp["q_a_norm"][0], 3)
    m["kv_a_norm"] = colvec(inp["kv_a_norm"][0], 2)
    m["w_q_b"] = np.ascontiguousarray(inp["w_q_b"][0], dtype=f32)
    m["w_kv_b"] = np.ascontiguousarray(inp["w_kv_b"][0], dtype=f32)
    for nm in ("g_q_mla", "g_k_mla", "g_q_swa", "g_k_swa", "g_q_mem", "g_k_mem", "swa_sink"):
        m[nm] = np.ascontiguousarray(np.asarray(inp[nm], f32).reshape(1, -1))
    for nm in ("w_mem_kv", "w_o_mla", "w_o_swa", "w_o_mem", "w_out", "w_up", "w_down"):
        m[nm] = np.ascontiguousarray(inp[nm][0], dtype=f32)
    m["conv_w"] = np.ascontiguousarray(np.asarray(inp["conv_w"][0], f32).T.reshape(NCH_FF, 128, 3).transpose(1, 0, 2))
    m["conv_b"] = colvec(inp["conv_b"][0], NCH_FF)
    return m


def run(inp, SP, SS, FT, n_cores=8, runner=None):
    nc = build_program(SP, SS, FT)
    in_maps = [make_core_inputs(c, inp, SP, SS) for c in range(n_cores)]
    if runner is None:
        res = run_bass_kernel_spmd(nc, in_maps, core_ids=list(range(n_cores))).results
    else:
        res = runner(nc, in_maps)
    QS = SS // 4
    yp = np.stack([np.asarray(res[c]["y0"], np.float32) for c in range(n_cores)], axis=0)
    ys = np.zeros((n_cores // 4, SS, D), np.float32)
    for c in range(n_cores):
        ys[c // 4, QS * (c % 4):QS * (c % 4 + 1)] = np.asarray(res[c]["y1"], np.float32)
    return yp, ys


def kernel(**inputs):
    inp = {k: np.asarray(v) for k, v in inputs.items()}
    yp, ys = run(inp, 8192, 16384, 512)
    return (yp, ys)
```

```python
import contextlib
import numpy as np
import ml_dtypes
import concourse.bass as bass
import concourse.mybir as mybir
from concourse.bass_utils import run_bass_kernel_spmd

F32 = mybir.dt.float32
BF16 = mybir.dt.bfloat16
ALU = mybir.AluOpType
AF = mybir.ActivationFunctionType
AX = mybir.AxisListType

EPOCH = 30000
N_DMA_SEMS = 40
N_SP_SEMS = 24
EPS = 1e-6
D = 1024
D_IN = 5024
D_FF = 2816
NCH_FF = 44
N_MEM = 256
NEG = -30000.0


class Op:
    __slots__ = ("eng", "fn", "deps", "is_dma", "sem", "val", "need_inc", "cnt")

    def __init__(self, eng, fn, is_dma):
        self.eng = eng
        self.fn = fn
        self.deps = []
        self.is_dma = is_dma
        self.sem = None
        self.val = None
        self.need_inc = False
        self.cnt = None


class Buf:
    __slots__ = ("name", "last_w", "readers", "dma_readers", "excl")

    def __init__(self, name, excl=False):
        self.name = name
        self.excl = excl
        self.last_w = None
        self.readers = {}
        self.dma_readers = []


class Prog:
    ENGS = ("pe", "act", "dve", "pool", "sp")

    def __init__(self, nc):
        self.nc = nc
        self.ops = {e: [] for e in self.ENGS}
        self.dma_rr_e = {e: 0 for e in self.ENGS}
        self.dma_last = [None] * N_DMA_SEMS
        self.dma_count = [0] * N_DMA_SEMS
        self.pending = {e: [] for e in self.ENGS}
        self.cap = None
        self.est = 0.3

    def replay_interleaved(self, lists):
        items = []
        for ci, l in enumerate(lists):
            wfin = {}
            rfin = {}
            for k, it in enumerate(l):
                if it[1] == "add":
                    reads, writes = it[4], it[5]
                else:
                    reads, writes = it[5], it[6]
                t = 0.0
                for b in reads:
                    t = max(t, wfin.get(id(b), 0.0))
                    if b.excl:
                        t = max(t, rfin.get(id(b), 0.0))
                for b in writes:
                    t = max(t, wfin.get(id(b), 0.0), rfin.get(id(b), 0.0))
                fin = t + it[0]
                for b in reads:
                    rfin[id(b)] = max(rfin.get(id(b), 0.0), fin)
                    if b.excl:
                        wfin[id(b)] = max(wfin.get(id(b), 0.0), fin)
                for b in writes:
                    wfin[id(b)] = max(wfin.get(id(b), 0.0), fin)
                items.append((t, ci, k, it))
        items.sort(key=lambda x: (x[0], x[1], x[2]))
        for _, _, _, it in items:
            if it[1] == "add":
                self.add(*it[2:])
            else:
                self.dma(it[2], it[3], it[4], reads=it[5], writes=it[6], slow=it[7])

    def barrier(self):
        deps = []
        for e in self.ENGS:
            for op in reversed(self.ops[e]):
                if not op.is_dma:
                    deps.append(op)
                    break
        deps += [d for d in self.dma_last if d is not None]
        for d in deps:
            d.need_inc = True
        for e in self.ENGS:
            self.pending[e] = list(deps)

    def _deps(self, op, reads, writes):
        ex = [b for b in reads if b.excl]
        if ex:
            reads = [b for b in reads if not b.excl]
            writes = list(writes) + [b for b in ex if b not in writes]
        deps = []
        for b in reads:
            if b.last_w is not None:
                deps.append((b.last_w, "raw"))
        for b in writes:
            if b.last_w is not None:
                deps.append((b.last_w, "waw"))
            for r in b.readers.values():
                deps.append((r, "war"))
            for r in b.dma_readers:
                deps.append((r, "war"))
        out = []
        seen = set()
        for d, kind in deps:
            if d is op or id(d) in seen:
                continue
            if (not d.is_dma) and (not op.is_dma) and d.eng == op.eng:
                if op.eng == "pe":
                    continue
            seen.add(id(d))
            out.append(d)
        if self.pending[op.eng]:
            for d in self.pending[op.eng]:
                if id(d) not in seen and not (d.eng == op.eng and not d.is_dma and not op.is_dma):
                    seen.add(id(d))
                    out.append(d)
            self.pending[op.eng] = []
        op.deps = out
        for d in out:
            d.need_inc = True
        for b in reads:
            if op.is_dma:
                b.dma_readers.append(op)
            else:
                b.readers[op.eng] = op
        for b in writes:
            b.last_w = op
            b.readers = {}
            b.dma_readers = []

    def add(self, eng, fn, reads=(), writes=()):
        if self.cap is not None:
            self.cap.append((self.est, "add", eng, fn, reads, writes))
            return None
        op = Op(eng, fn, False)
        self._deps(op, reads, writes)
        self.ops[eng].append(op)
        return op

    def dma(self, eng, out, in_, reads=(), writes=(), slow=False):
        if self.cap is not None:
            self.cap.append((1.5, "dma", eng, out, in_, reads, writes, slow))
            return None

        def fn(e):
            if slow:
                return e.dma_start(out=out, in_=in_, allow_slow_non_contiguous=True)
            return e.dma_start(out=out, in_=in_)
        op = Op(eng, fn, True)
        self._deps(op, reads, writes)
        lo, hi = (0, N_SP_SEMS) if eng == "sp" else (N_SP_SEMS, N_DMA_SEMS)
        s = lo + self.dma_rr_e[eng] % (hi - lo)
        self.dma_rr_e[eng] += 1
        prev = self.dma_last[s]
        if prev is not None and prev not in op.deps:
            op.deps.append(prev)
        self.dma_count[s] += 1
        op.sem = ("dma", s)
        op.val = 16 * self.dma_count[s]
        self.dma_last[s] = op
        self.ops[eng].append(op)
        return op

    def emit(self):
        nc = self.nc
        n_ep = {}
        for e in self.ENGS:
            c = 0
            for op in self.ops[e]:
                if not op.is_dma and op.need_inc:
                    c += 1
                    op.cnt = c
                    op.sem = (e, (c - 1) // EPOCH)
                    op.val = (c - 1) % EPOCH + 1
            n_ep[e] = max(1, (c + EPOCH - 1) // EPOCH)
        with contextlib.ExitStack() as st:
            sems = {}
            for e in self.ENGS:
                for k in range(n_ep[e]):
                    sems[(e, k)] = st.enter_context(nc.semaphore(f"s_{e}_{k}"))
            for s in range(N_DMA_SEMS):
                sems[("dma", s)] = st.enter_context(nc.semaphore(f"s_dma_{s}"))
            block = st.enter_context(nc.Block())
            engmap = {"pe": block.tensor, "act": block.scalar, "dve": block.vector,
                      "pool": block.gpsimd, "sp": block.sync}

            def make(ename):
                ops = self.ops[ename]

                def body(e):
                    waited = {}
                    for op in ops:
                        for d in op.deps:
                            if waited.get(d.sem, 0) >= d.val:
                                continue
                            e.wait_ge(sems[d.sem], d.val)
                            waited[d.sem] = d.val
                            if d.sem[0] != "dma":
                                for k in range(d.sem[1]):
                                    waited[(d.sem[0], k)] = EPOCH
                        inst = op.fn(e)
                        if op.is_dma:
                            inst.then_inc(sems[op.sem], 16)
                        elif op.need_inc:
                            inst.then_inc(sems[op.sem], 1)
                    if ename == "sp":
                        for s in range(N_DMA_SEMS):
                            if self.dma_count[s]:
                                e.wait_ge(sems[("dma", s)], 16 * self.dma_count[s])
                return body

            for ename in self.ENGS:
                if self.ops[ename] or ename == "sp":
                    engmap[ename](make(ename))


class T:
    def __init__(self, h, name, excl=False):
        self.h = h
        self.b = Buf(name, excl)

    def __getitem__(self, k):
        return self.h[k]


def build_program(SP, SS, FT):
    QS = SS // 4
    jobs = [
        dict(S=SP, KB=SP // 128, qoff=0, NQ=SP // 128, f0=0, fn=SP // 128, qg0=0),
        dict(S=SS, KB=SS // 128, qoff=1, NQ=QS // 128 + 2, f0=1, fn=QS // 128, qg0=SP // 128),
    ]
    NQT = sum(j["NQ"] for j in jobs)
    nc = bass.Bass("TRN2", target_bir_lowering=False)

    def din(name, shape, dt=F32):
        return nc.dram_tensor(name, list(shape), dt, kind="ExternalInput").ap()

    def dscr(name, shape, dt=BF16):
        return nc.dram_tensor(name, list(shape), dt, kind="Internal").ap()

    xkv = [din("xkv0", [SP, D]), din("xkv1", [SS, D])]
    mem = [din("mem0", [N_MEM, D]), din("mem1", [N_MEM, D])]
    rope_d = [din(f"rope{j}", [jobs[j]["KB"], 128, 96]) for j in range(2)]
    hval_d = din("hval", [128, NQT])
    swab_d = din("swab", [128, NQT * 3])
    bandm_d = din("bandm", [128, 2, 128], BF16)
    ident_d = din("ident", [128, 128], BF16)
    w_in_d = din("w_in", [D, D_IN])
    g_mix_d = din("g_mix", [128, 8])
    g_mem_d = din("g_mem", [128, 8])
    g_ffn_d = din("g_ffn", [128, 8])
    qan_d = din("q_a_norm", [128, 3])
    kvan_d = din("kv_a_norm", [128, 2])
    w_qb_d = din("w_q_b", [384, 768])
    w_kvb_d = din("w_kv_b", [256, 1024])
    gqm_d = din("g_q_mla", [1, 96])
    gkm_d = din("g_k_mla", [1, 96])
    gqs_d = din("g_q_swa", [1, 64])
    gks_d = din("g_k_swa", [1, 64])
    gqe_d = din("g_q_mem", [1, 128])
    gke_d = din("g_k_mem", [1, 128])
    sink_d = din("swa_sink", [1, 8])
    w_memkv_d = din("w_mem_kv", [D, 1024])
    w_omla_d = din("w_o_mla", [512, D])
    w_oswa_d = din("w_o_swa", [512, D])
    w_omem_d = din("w_o_mem", [512, D])
    w_out_d = din("w_out", [D, D])
    w_up_d = din("w_up", [D, 2 * D_FF])
    convw_d = din("conv_w", [128, NCH_FF, 3])
    convb_d = din("conv_b", [128, NCH_FF])
    w_down_d = din("w_down", [D_FF, D])
    y_d = [nc.dram_tensor("y0", [SP, D], F32, kind="ExternalOutput").ap(),
           nc.dram_tensor("y1", [QS, D], F32, kind="ExternalOutput").ap()]
    KT_d = [dscr(f"KT{j}", [8, 96, jobs[j]["S"]]) for j in range(2)]
    V_d = [dscr(f"V{j}", [8, 128, jobs[j]["KB"], 128]) for j in range(2)]
    KsT_d = [dscr(f"KsT{j}", [2, 64, (jobs[j]["KB"] + 2) * 128]) for j in range(2)]
    Vs_d = [dscr(f"Vs{j}", [jobs[j]["KB"] + 2, 128, 256]) for j in range(2)]
    QT_d = [dscr(f"QT{j}", [8, 96, jobs[j]["NQ"] * 128]) for j in range(2)]
    OT_d = [dscr(f"OT{j}", [4, 128, jobs[j]["NQ"] * 128]) for j in range(2)]
    OS_d = [dscr(f"OS{j}", [4, 128, jobs[j]["NQ"] * 128]) for j in range(2)]
    OM_d = [dscr(f"OM{j}", [4, 128, jobs[j]["NQ"] * 128]) for j in range(2)]
    XN_d = [dscr(f"XN{j}", [jobs[j]["NQ"] * 128, D], F32) for j in range(2)]
    H2_d = [dscr(f"H2{j}", [8, 128, jobs[j]["NQ"] * 128 + 2]) for j in range(2)]
    HT_d = [dscr(f"HT{j}", [8, 128, jobs[j]["NQ"] * 128]) for j in range(2)]
    dbuf = {}

    def DB(*key):
        return Buf("scratch")

    P = Prog(nc)
    root = contextlib.ExitStack()

    def sbuf(st, name, shape, dt):
        return T(st.enter_context(nc.sbuf_tensor("sb_" + name, list(shape), dt)), name)

    def bl(ts_):
        return [t if isinstance(t, Buf) else t.b for t in ts_]

    def fsz(ap):
        n = 1
        for d_ in ap.shape[1:]:
            n *= int(d_)
        return n

    def mm(out, lhsT, rhs, start, stop, r, w):
        P.est = 0.03 + fsz(rhs) / 2400.0 * (4.0 if rhs.dtype == F32 else 1.0)
        P.add("pe", lambda e: e.matmul(out, lhsT, rhs, start=start, stop=stop), reads=bl(r), writes=bl(w))

    def tr(out, in_, idn, r, w):
        P.est = 0.07
        P.add("pe", lambda e: e.transpose(out, in_, idn), reads=bl(r), writes=bl(w))

    def vest(eng, out):
        n = fsz(out)
        return (0.35 + n / 480.0) if eng == "pool" else (0.2 + n / 900.0)

    def tt(eng, out, in0, in1, op, r, w):
        P.est = vest(eng, out)
        P.add(eng, lambda e: e.tensor_tensor(out, in0, in1, op), reads=bl(r), writes=bl(w))

    def ts(eng, out, in0, s1, s2, op0, op1, r, w):
        P.est = vest(eng, out)
        if op1 is None:
            P.add(eng, lambda e: e.tensor_scalar(out, in0, s1, None, op0), reads=bl(r), writes=bl(w))
        else:
            P.add(eng, lambda e: e.tensor_scalar(out, in0, s1, s2, op0, op1), reads=bl(r), writes=bl(w))

    def stt(eng, out, in0, sc, in1, op0, op1, r, w):
        P.est = vest(eng, out)
        P.add(eng, lambda e: e.scalar_tensor_tensor(out, in0, sc, in1, op0, op1), reads=bl(r), writes=bl(w))

    def cp(eng, out, in_, r, w):
        P.est = (0.3 + fsz(out) / 1150.0) if eng == "act" else vest(eng, out)
        if eng == "act":
            P.add("act", lambda e: e.activation(out, in_, AF.Copy), reads=bl(r), writes=bl(w))
        else:
            P.add(eng, lambda e: e.tensor_copy(out, in_), reads=bl(r), writes=bl(w))

    def act(out, in_, func, r, w, **kw):
        P.est = 0.3 + fsz(out) / 1150.0
        P.add("act", lambda e: e.activation(out, in_, func, **kw), reads=bl(r), writes=bl(w))

    def memset(eng, ap, val, w):
        P.add(eng, lambda e: e.memset(ap, val), writes=bl(w))

    def red(out, in_, r, w):
        P.est = 0.2 + fsz(in_) / 900.0
        P.add("dve", lambda e: e.tensor_reduce(out, in_, AX.X, ALU.add), reads=bl(r), writes=bl(w))

    def recip(out, in_, r, w):
        P.est = 0.3 + fsz(out) * 0.0048
        P.add("dve", lambda e: e.reciprocal(out, in_), reads=bl(r), writes=bl(w))

    def dma(eng, out, in_, r, w, slow=False):
        P.dma(eng, out, in_, reads=bl(r), writes=bl(w), slow=slow)

    with root:
        PD = [root.enter_context(nc.psum_tensor(f"pd{i}", [128, 1024], F32)) for i in range(4)]

        class PBank:
            def __init__(self, i):
                self.h = PD[i // 2]
                self.off = (i % 2) * 512
                self.b = Buf(f"pb{i}", True)

            def __getitem__(self, k):
                return self.h[:, self.off:self.off + 512][k]

        PB = [PBank(i) for i in range(8)]

        def pbf(i):
            return PB[i][:].bitcast(BF16)

        ident = sbuf(root, "ident", [128, 128], BF16)
        bandm = sbuf(root, "bandm", [128, 2, 128], BF16)
        hval = sbuf(root, "hval", [128, NQT], F32)
        swab = sbuf(root, "swab", [128, NQT * 3], F32)
        e128 = sbuf(root, "e128", [1, 128], F32)
        sinkrow = sbuf(root, "sinkrow", [1, 8 * 128], F32)
        sink_t = sbuf(root, "sink_t", [1, 8], F32)
        gt = {}
        for nm, dd, w_ in (("gqm", gqm_d, 96), ("gkm", gkm_d, 96), ("gqs", gqs_d, 64), ("gks", gks_d, 64),
                           ("gqe", gqe_d, 128), ("gke", gke_d, 128)):
            gt[nm] = sbuf(root, nm, [128, w_], F32)
            dma("sp", gt[nm][:], dd.partition_broadcast(128), [], [gt[nm]])
        dma("sp", ident[:], ident_d, [], [ident])
        dma("sp", bandm[:], bandm_d, [], [bandm])
        dma("sp", hval[:], hval_d, [], [hval])
        dma("sp", swab[:], swab_d, [], [swab])
        dma("sp", sink_t[:], sink_d, [], [sink_t])
        memset("dve", e128[:], 0.0, [e128])
        memset("dve", e128[0:1, 64:128], 1.0, [e128])
        act(sink_t[:], sink_t[:], AF.Exp, [sink_t], [sink_t])
        cp("dve", sinkrow[:].rearrange("p (h t) -> p h t", h=8), sink_t[:].unsqueeze(2).to_broadcast([1, 8, 128]),
           [sink_t], [sinkrow])

        def load_weight(w, src, nchunk, ncols, stage, gain=None, pk=128, cpiece=None):
            cpiece = cpiece or ncols
            stages = stage if isinstance(stage, list) else [stage]
            k = 0
            for c in range(nchunk):
                for c0 in range(0, ncols, cpiece):
                    n = min(cpiece, ncols - c0)
                    st_ = stages[k % len(stages)]
                    k += 1
                    dma("sp", st_[0:pk, 0:n], src[c * pk:(c + 1) * pk, c0:c0 + n], [], [st_])
                    if gain is not None:
                        ts("dve", w[:, c, c0:c0 + n], st_[0:pk, 0:n], gain[:, c:c + 1], None, ALU.mult, None, [st_, gain], [w])
                    else:
                        cp("dve", w[:, c, c0:c0 + n], st_[0:pk, 0:n], [st_], [w])

        I32 = mybir.dt.int32

        def rstd_dve(ssq, n, dim, rstd, tn):
            v = ssq[:, 0:n]
            y = rstd[:, 0:n]
            t_ = tn[:, 0:n]
            ts("dve", v, v, 1.0 / dim, EPS, ALU.mult, ALU.add, [ssq], [ssq])
            P.add("dve", lambda e: e.tensor_single_scalar(y.bitcast(I32), v.bitcast(I32), 1, ALU.logical_shift_right),
                  reads=[ssq.b], writes=[rstd.b])
            P.add("dve", lambda e: e.tensor_scalar(y.bitcast(I32), y.bitcast(I32), -1, 0x5f3759df, ALU.mult, ALU.add),
                  reads=[rstd.b], writes=[rstd.b])
            for _ in range(2):
                tt("dve", t_, y, y, ALU.mult, [rstd], [tn])
                tt("dve", t_, t_, v, ALU.mult, [tn, ssq], [tn])
                ts("dve", t_, t_, -0.5, 1.5, ALU.mult, ALU.add, [tn], [tn])
                tt("dve", y, y, t_, ALU.mult, [rstd, tn], [rstd])

        def rstd_from_ssq(ssq, n, dim, rstd):
            act(rstd[:, 0:n], ssq[:, 0:n], AF.Ln, [ssq], [rstd], scale=1.0 / dim, bias=EPS)
            act(rstd[:, 0:n], rstd[:, 0:n], AF.Exp, [rstd], [rstd], scale=-0.5)

        def transposes(srcs, cols, pbank, dst_ap, dst_t, src_ts, evac="dve", slot0=0):
            n = len(srcs)
            pv = pbf(pbank).rearrange("p (c t) -> p c t", c=8)
            for c in range(n):
                tr(pv[0:cols, slot0 + c, :], srcs[c], ident[:], src_ts + [ident], [PB[pbank]])
            cp(evac, dst_ap, pv[0:cols, slot0:slot0 + n, :], [PB[pbank]], [dst_t])

        def headnorm(src_ap, src_ts, H, Dh, gain, out_ap, out_t, tmp, sq, ssq, rstd, rope=None):
            tv = tmp[:, 0:H, 0:Dh]
            sv = sq[:, 0:H, 0:Dh]
            act(sv, src_ap, AF.Square, src_ts, [sq])
            red(ssq[:, 0:H], sv, [sq], [ssq])
            rstd_from_ssq(ssq, H, Dh, rstd)
            tt("dve", tv, src_ap, rstd[:, 0:H].unsqueeze(2).to_broadcast([128, H, Dh]), ALU.mult, src_ts + [rstd], [tmp])
            gb = gain[:, 0:Dh].unsqueeze(1).to_broadcast([128, H, Dh])
            if rope is None:
                tt("dve", out_ap, tv, gb, ALU.mult, [tmp, gain], [out_t])
                return
            tab, r, tab_t = rope
            n0 = Dh - 2 * r
            tt("dve", tv, tv, gb, ALU.mult, [tmp, gain], [tmp])
            if n0 > 0:
                cp("dve", out_ap[:, :, 0:n0], tmp[:, 0:H, 0:n0], [tmp], [out_t])
            x1 = tmp[:, 0:H, n0:n0 + r]
            x2 = tmp[:, 0:H, n0 + r:Dh]
            cb = tab[:, 0:r].unsqueeze(1).to_broadcast([128, H, r])
            sb_ = tab[:, r:2 * r].unsqueeze(1).to_broadcast([128, H, r])
            a = sq[:, 0:H, 0:r]
            b = sq[:, 0:H, r:2 * r]
            c_ = sq[:, 0:H, 2 * r:3 * r]
            d_ = sq[:, 0:H, 3 * r:4 * r]
            tt("dve", a, x1, cb, ALU.mult, [tmp, tab_t], [sq])
            tt("dve", b, x2, sb_, ALU.mult, [tmp, tab_t], [sq])
            tt("dve", c_, x2, cb, ALU.mult, [tmp, tab_t], [sq])
            tt("dve", d_, x1, sb_, ALU.mult, [tmp, tab_t], [sq])
            tt("dve", out_ap[:, :, n0:n0 + r], a, b, ALU.subtract, [sq], [out_t])
            tt("dve", out_ap[:, :, n0 + r:Dh], c_, d_, ALU.add, [sq], [out_t])

        def rms_block(x_t, junk, ssq, rstd, h, extra_scale=None):
            act(junk[:], x_t[:], AF.Square, [x_t], [junk, ssq], accum_out=ssq[:, 0:1])
            rstd_from_ssq(ssq, 1, D, rstd)
            if extra_scale is not None:
                tt("dve", rstd[:, 0:1], rstd[:, 0:1], extra_scale, ALU.mult, [rstd, hval], [rstd])
            ts("dve", h[:], x_t[:], rstd[:, 0:1], None, ALU.mult, None, [x_t, rstd], [h])

        def pipeline(n, levels):
            nl = len(levels)
            for i in range(n + nl - 1):
                lists = []
                for k in range(nl - 1, -1, -1):
                    b_ = i - k
                    if 0 <= b_ < n:
                        for fn in levels[k]:
                            P.cap = []
                            fn(b_)
                            lists.append(P.cap)
                            P.cap = None
                P.replay_interleaved(lists)

        def load_cols(w, dst0, src, col0, ncols, nchunk, stage, gain):
            stages = stage if isinstance(stage, list) else [stage]
            for c in range(nchunk):
                st_ = stages[c % len(stages)]
                dma("sp", st_[:, 0:ncols], src[c * 128:(c + 1) * 128, col0:col0 + ncols], [], [st_])
                ts("dve", w[:, c, dst0:dst0 + ncols], st_[:, 0:ncols], gain[:, c:c + 1], None, ALU.mult, None, [st_, gain], [w])

        def front(x_t, hbt, ssq_, rstd_, tbank, hTt, extra_scale=None, tn=None, evac="dve"):
            act(hbt[:], x_t[:], AF.Square, [x_t], [hbt, ssq_], accum_out=ssq_[:, 0:1])
            if tn is not None:
                rstd_dve(ssq_, 1, D, rstd_, tn)
            else:
                rstd_from_ssq(ssq_, 1, D, rstd_)
            if extra_scale is not None:
                tt("dve", rstd_[:, 0:1], rstd_[:, 0:1], extra_scale, ALU.mult, [rstd_, hval], [rstd_])
            ts("dve", hbt[:], x_t[:], rstd_[:, 0:1], None, ALU.mult, None, [x_t, rstd_], [hbt])
            transposes([hbt[:, c * 128:(c + 1) * 128] for c in range(8)], 128, tbank, hTt[:], hTt, [hbt], evac=evac)

        def proj(pbank, col0, ncols, lhs, wmat):
            for c in range(8):
                mm(PB[pbank][:, 0:ncols], lhs[:, c, :], wmat[:, c, col0:col0 + ncols], c == 0, c == 7, [lhs, wmat], [PB[pbank]])

        def norm_rep(src_t, src_ap_o, src_ap_d, n, dst_ap, dst_t, rr):
            recip(rr[0:64, 0:n], src_ap_d, [src_t], [rr])
            tt("dve", dst_ap, src_ap_o, rr[0:64, 0:n], ALU.mult, [src_t, rr], [dst_t])

        with contextlib.ExitStack() as s1:
            g_mix = sbuf(s1, "g_mix", [128, 8], F32)
            g_mem = sbuf(s1, "g_mem", [128, 8], F32)
            qan = sbuf(s1, "qan", [128, 3], F32)
            kvan = sbuf(s1, "kvan", [128, 2], F32)
            for t_, d_ in ((g_mix, g_mix_d), (g_mem, g_mem_d), (qan, qan_d), (kvan, kvan_d)):
                dma("sp", t_[:], d_, [], [t_])
            xt = [sbuf(s1, f"xt{i}", [128, D], F32) for i in range(2)]
            ssq1 = sbuf(s1, "ssq1", [128, 8], F32)
            rstd1 = sbuf(s1, "rstd1", [128, 8], F32)
            ssqL = sbuf(s1, "ssqL", [128, 8], F32)
            rstdL = sbuf(s1, "rstdL", [128, 8], F32)
            hb = sbuf(s1, "hb", [128, D], BF16)
            hT = [sbuf(s1, f"hT{i}", [128, 8, 128], BF16) for i in range(2)]
            nb = sbuf(s1, "nb", [128, 384], BF16)
            nbT = sbuf(s1, "nbT", [128, 3, 128], BF16)
            ropet = [sbuf(s1, f"ropet{i}", [128, 96], F32) for i in range(3)]
            tmpA = sbuf(s1, "tmpA", [128, 8, 96], F32)
            sqA = sbuf(s1, "sqA", [128, 8, 128], F32)
            ssqA = sbuf(s1, "ssqA", [128, 8], F32)
            rstdA = sbuf(s1, "rstdA", [128, 8], F32)
            tmpS = sbuf(s1, "tmpS", [128, 8, 64], F32)
            sqS = sbuf(s1, "sqS", [128, 8, 128], F32)
            ssqS = sbuf(s1, "ssqS", [128, 8], F32)
            rstdS = sbuf(s1, "rstdS", [128, 8], F32)
            kq = sbuf(s1, "kq", [128, 8, 96], BF16)
            kqT = [sbuf(s1, f"kqT{i}", [96, 8, 128], BF16) for i in range(2)]
            ks = sbuf(s1, "ks", [128, 8, 64], BF16)
            ksT = [sbuf(s1, f"ksT{i}", [64, 8 * 128], BF16) for i in range(2)]
            KmT = [sbuf(s1, f"KmT{j}", [128, 4, 256], BF16) for j in range(2)]
            Vm = [sbuf(s1, f"Vm{j}", [128, 2, 512], BF16) for j in range(2)]

            def lora(pbank_in, ncol, nch, w2, outbanks, ncols_out, tbank):
                src = PB[pbank_in][:, 0:ncol]
                act(nb[:, 0:ncol], src, AF.Square, [PB[pbank_in]], [nb, ssqL], accum_out=ssqL[:, 0:1])
                rstd_from_ssq(ssqL, 1, ncol, rstdL)
                ts("dve", nb[:, 0:ncol], src, rstdL[:, 0:1], None, ALU.mult, None, [PB[pbank_in], rstdL], [nb])
                transposes([nb[:, c * 128:(c + 1) * 128] for c in range(nch)], 128, tbank, nbT[:, 0:nch, :], nbT, [nb])
                for oi, ob in enumerate(outbanks):
                    for c in range(nch):
                        mm(PB[ob][:, 0:ncols_out], nbT[:, c, :], w2[:, c, oi * ncols_out:(oi + 1) * ncols_out],
                           c == 0, c == nch - 1, [nbT, w2], [PB[ob]])

            with contextlib.ExitStack() as sA:
                w_inA = sbuf(sA, "w_inA", [128, 8, 544], BF16)
                w_kvb = sbuf(sA, "w_kvb", [128, 2, 1024], BF16)
                with contextlib.ExitStack() as sL:
                    stageA = [sbuf(sL, "stageA%d" % i_, [128, 1024], F32) for i_ in range(2)]
                    load_cols(w_inA, 0, w_in_d, 384, 288, 8, stageA, g_mix)
                    load_cols(w_inA, 288, w_in_d, 1184, 256, 8, stageA, g_mix)
                    load_weight(w_kvb, w_kvb_d, 2, 1024, stageA, gain=kvan)
                P.barrier()
                kcat = [sbuf(sA, f"kcat{i}", [128, 8, 96], F32) for i in range(2)]
                zks = [sbuf(sA, f"zks{i}", [128, 2, 64], F32) for i in range(2)]
                vsb = [sbuf(sA, f"vsb{i}", [128, 8, 128], BF16) for i in range(2)]
                vss = [sbuf(sA, f"vss{i}", [128, 2, 128], BF16) for i in range(2)]
                zeros = sbuf(sA, "zeros", [128, 512], BF16)
                memset("dve", zeros[:], 0.0, [zeros])
                for i in range(2):
                    memset("dve", vsb[i][:], 1.0, [vsb[i]])
                    memset("dve", vss[i][:], 1.0, [vss[i]])
                for j in range(2):
                    KB = jobs[j]["KB"]
                    for blk in (0, KB + 1):
                        dma("pool", KsT_d[j][:, :, blk * 128:(blk + 1) * 128].rearrange("h d t -> d h t"),
                            zeros[0:64, 0:256].rearrange("p (h t) -> p h t", h=2), [zeros], [])
                        dma("pool", Vs_d[j][blk], zeros[:, 0:256], [zeros], [])
                    ncol = jobs[j]["NQ"] * 128 + 2
                    for col in (0, ncol - 1):
                        dma("pool", H2_d[j][:, :, col:col + 1].rearrange("c p o -> p c o"),
                            zeros[:, 0:8].rearrange("p (c o) -> p c o", o=1), [zeros], [], slow=True)
                blocksA = [(j, kb) for j in range(2) for kb in range(jobs[j]["KB"])]

                def A0(bi):
                    j, kb = blocksA[bi]
                    x_t = xt[bi % 2]
                    dma("sp", x_t[:], xkv[j][kb * 128:(kb + 1) * 128, :], [], [x_t])
                    dma("sp", ropet[bi % 3][:], rope_d[j][kb], [], [ropet[bi % 3]])
                    front(x_t, hb, ssq1, rstd1, 0, hT[bi % 2], evac="act")
                    q_ = kb - jobs[j]["qoff"]
                    if 0 <= q_ < jobs[j]["NQ"]:
                        dma("pool", HT_d[j][:, :, q_ * 128:(q_ + 1) * 128].rearrange("c p t -> p c t"), hT[bi % 2][:], [hT[bi % 2]], [])

                swa_last = [jobs[0]["KB"], jobs[1]["qoff"] + jobs[1]["NQ"] + 1]

                def A1(bi):
                    j, kb = blocksA[bi]
                    xi = bi % 2
                    proj(1, 0, 288, hT[xi], w_inA)
                    if kb <= swa_last[j]:
                        proj(2, 288, 256, hT[xi], w_inA)
                    lora(1, 256, 2, w_kvb, [3, 4], 512, 5)
                    cp("dve", kcat[xi][:, :, 64:96], PB[1][:, 256:288].unsqueeze(1).to_broadcast([128, 8, 32]), [PB[1]], [kcat[xi]])
                    if kb <= swa_last[j]:
                        cp("act", zks[xi][:], PB[2][:, 0:128].rearrange("p (h d) -> p h d", h=2), [PB[2]], [zks[xi]])
                        cp("act", vss[xi][:, :, 0:64], PB[2][:, 128:256].rearrange("p (h d) -> p h d", h=2), [PB[2]], [vss[xi]])
                    for half in range(2):
                        kv3 = PB[3 + half][:].rearrange("p (h d) -> p h d", h=4)
                        cp("dve", kcat[xi][:, half * 4:(half + 1) * 4, 0:64], kv3[:, :, 0:64], [PB[3 + half]], [kcat[xi]])
                        cp("act", vsb[xi][:, half * 4:(half + 1) * 4, 0:64], kv3[:, :, 64:128], [PB[3 + half]], [vsb[xi]])

                def A2a(bi):
                    j, kb = blocksA[bi]
                    xi = bi % 2
                    rt = ropet[bi % 3]
                    dma("pool", V_d[j][:, :, kb, :].rearrange("h p e -> p h e"), vsb[xi][:], [vsb[xi]], [])
                    headnorm(kcat[xi][:], [kcat[xi]], 8, 96, gt["gkm"], kq[:], kq, tmpA, sqA, ssqA, rstdA, rope=(rt[:, 0:32], 16, rt))
                    transposes([kq[:, h, :] for h in range(8)], 96, 6, kqT[xi][:], kqT[xi], [kq], evac="act")
                    dma("pool", KT_d[j][:, :, kb * 128:(kb + 1) * 128].rearrange("h d t -> d h t"), kqT[xi][:], [kqT[xi]], [])

                def A2b(bi):
                    j, kb = blocksA[bi]
                    if kb > swa_last[j]:
                        return
                    xi = bi % 2
                    rt = ropet[bi % 3]
                    dma("pool", Vs_d[j][kb + 1], vss[xi][:].rearrange("p h e -> p (h e)"), [vss[xi]], [])
                    headnorm(zks[xi][:], [zks[xi]], 2, 64, gt["gks"], ks[:, 0:2, :], ks, tmpS, sqS, ssqS, rstdS, rope=(rt[:, 32:96], 32, rt))
                    kv_ = ksT[xi][:, 0:256].rearrange("p (h t) -> p h t", h=2)
                    transposes([ks[:, h, :] for h in range(2)], 64, 7, kv_, ksT[xi], [ks], evac="act")
                    dma("pool", KsT_d[j][:, :, (kb + 1) * 128:(kb + 2) * 128].rearrange("h d t -> d h t"), kv_, [ksT[xi]], [])

                pipeline(len(blocksA), [[A0], [A1], [A2a, A2b]])
            P.barrier()
            with contextlib.ExitStack() as sM:
                stageM = sbuf(sM, "stageM", [128, 1024], F32)
                w_memkv = sbuf(sM, "w_memkv", [128, 8, 1024], BF16)
                load_weight(w_memkv, w_memkv_d, 8, 1024, stageM, gain=g_mem)
                kmn = sbuf(sM, "kmn", [128, 4, 128], BF16)
                sqm = sbuf(sM, "sqm", [128, 4, 128], F32)
                tmpm = sbuf(sM, "tmpm", [128, 4, 128], F32)
                for j in range(2):
                    for mb in range(2):
                        x_t = xt[mb]
                        dma("sp", x_t[:], mem[j][mb * 128:(mb + 1) * 128, :], [], [x_t])
                        front(x_t, hb, ssq1, rstd1, 0, hT[0])
                        proj(1, 0, 512, hT[0], w_memkv)
                        proj(2, 512, 512, hT[0], w_memkv)
                        headnorm(PB[1][:].rearrange("p (h d) -> p h d", h=4), [PB[1]], 4, 128, gt["gke"], kmn[:], kmn,
                                 tmpm, sqm, ssqA, rstdA)
                        transposes([kmn[:, h, :] for h in range(4)], 128, 0, KmT[j][:, :, mb * 128:(mb + 1) * 128], KmT[j], [kmn])
                        cp("act", Vm[j][:, mb, :], PB[2][:], [PB[2]], [Vm[j]])
            P.barrier()
            with contextlib.ExitStack() as sB:
                w_inB = sbuf(sB, "w_inB", [128, 8, 1408], BF16)
                w_qb = sbuf(sB, "w_qb", [128, 3, 768], BF16)
                with contextlib.ExitStack() as sL:
                    stageB = [sbuf(sL, "stageB%d" % i_, [128, 1024], F32) for i_ in range(2)]
                    load_cols(w_inB, 0, w_in_d, 0, 384, 8, stageB, g_mix)
                    load_cols(w_inB, 384, w_in_d, 672, 512, 8, stageB, g_mix)
                    load_cols(w_inB, 896, w_in_d, 1440, 512, 8, stageB, g_mix)
                    load_weight(w_qb, w_qb_d, 3, 768, stageB, gain=qan)
                P.barrier()
                zq = [sbuf(sB, f"zq{i}", [128, 8, 96], F32) for i in range(2)]
                zs = [sbuf(sB, f"zs{i}", [128, 8, 64], F32) for i in range(2)]
                zm = [sbuf(sB, f"zm{i}", [128, 4, 128], F32) for i in range(2)]
                sqM = sbuf(sB, "sqM", [128, 4, 128], F32)
                ssqM = sbuf(sB, "ssqM", [128, 8], F32)
                rstdM = sbuf(sB, "rstdM", [128, 8], F32)
                qm = sbuf(sB, "qm", [128, 4, 128], BF16)
                qmT = [sbuf(sB, f"qmT{i}", [128, 4, 128], BF16) for i in range(2)]
                kst = [sbuf(sB, f"kst{i}", [64, 2, 384], BF16) for i in range(2)]
                vst = [sbuf(sB, f"vst{i}", [128, 3, 256], BF16) for i in range(2)]
                pt = [sbuf(sB, f"pt{i}", [128, 512], BF16) for i in range(3)]
                rrS = sbuf(sB, "rrS", [64, 512], F32)
                osT = [sbuf(sB, f"osT{i}", [64, 8 * 128], BF16) for i in range(2)]
                pm = sbuf(sB, "pm", [128, 256], BF16)
                pmT = sbuf(sB, "pmT", [128, 2, 128], BF16)
                rsum = sbuf(sB, "rsum", [128, 4], F32)
                rrec = sbuf(sB, "rrec", [128, 4], F32)
                om = sbuf(sB, "om", [128, 4, 128], BF16)
                omT = [sbuf(sB, f"omT{i}", [128, 4, 128], BF16) for i in range(2)]
                blocksB = [(j, q) for j in range(2) for q in range(jobs[j]["NQ"])]

                def B01(bi):
                    j, q = blocksB[bi]
                    kblk = jobs[j]["qoff"] + q
                    xi = bi % 2
                    dma("sp", ropet[xi][:], rope_d[j][kblk], [], [ropet[xi]])
                    hTt = hT[xi]
                    dma("sp", hTt[:], HT_d[j][:, :, q * 128:(q + 1) * 128].rearrange("c p t -> p c t"), [], [hTt])
                    proj(1, 0, 384, hTt, w_inB)
                    lora(1, 384, 3, w_qb, [0, 1], 384, 0)
                    for half, bk in ((0, 0), (1, 1)):
                        cp("act" if half else "dve", zq[xi][:, half * 4:(half + 1) * 4, :],
                           PB[bk][:, 0:384].rearrange("p (h d) -> p h d", h=4), [PB[bk]], [zq[xi]])
                    proj(0, 384, 512, hTt, w_inB)
                    cp("act", zs[xi][:], PB[0][:].rearrange("p (h d) -> p h d", h=8), [PB[0]], [zs[xi]])
                    proj(1, 896, 512, hTt, w_inB)
                    cp("dve", zm[xi][:], PB[1][:].rearrange("p (h d) -> p h d", h=4), [PB[1]], [zm[xi]])

                def B2a(bi):
                    j, q = blocksB[bi]
                    xi = bi % 2
                    rt = ropet[xi]
                    headnorm(zq[xi][:], [zq[xi]], 8, 96, gt["gqm"], kq[:], kq, tmpA, sqA, ssqA, rstdA, rope=(rt[:, 0:32], 16, rt))
                    transposes([kq[:, h, :] for h in range(8)], 96, 2, kqT[xi][:], kqT[xi], [kq], evac="act")
                    dma("pool", QT_d[j][:, :, q * 128:(q + 1) * 128].rearrange("h d t -> d h t"), kqT[xi][:], [kqT[xi]], [])

                def B2bc(bi):
                    j, q = blocksB[bi]
                    xi = bi % 2
                    rt = ropet[xi]
                    kblk = jobs[j]["qoff"] + q
                    dma("sp", kst[xi][:], KsT_d[j][:, :, kblk * 128:(kblk + 3) * 128].rearrange("h d t -> d h t"), [], [kst[xi]])
                    dma("sp", vst[xi][:], Vs_d[j][kblk:kblk + 3].rearrange("b p e -> p b e"), [], [vst[xi]])
                    headnorm(zs[xi][:], [zs[xi]], 8, 64, gt["gqs"], ks[:], ks, tmpS, sqS, ssqS, rstdS, rope=(rt[:, 32:96], 32, rt))
                    transposes([ks[:, h, :] for h in range(8)], 64, 3, ksT[xi][:].rearrange("p (h t) -> p h t", h=8), ksT[xi], [ks], evac="act")
                    z = zm[xi]
                    sv = sqM[:]
                    act(sv, z[:], AF.Square, [z], [sqM])
                    red(ssqM[:, 0:4], sv, [sqM], [ssqM])
                    rstd_from_ssq(ssqM, 4, 128, rstdM)
                    tt("dve", sv, z[:], rstdM[:, 0:4].unsqueeze(2).to_broadcast([128, 4, 128]), ALU.mult, [z, rstdM], [sqM])
                    tt("dve", qm[:], sv, gt["gqe"][:, 0:128].unsqueeze(1).to_broadcast([128, 4, 128]), ALU.mult, [sqM, gt["gqe"]], [qm])
                    transposes([qm[:, h, :] for h in range(4)], 128, 3, qmT[xi][:], qmT[xi], [qm], evac="act")

                def B3a(bi):
                    j, q = blocksB[bi]
                    xi = bi % 2
                    qg = jobs[j]["qg0"] + q
                    ksTt = ksT[xi]
                    ost = osT[xi]
                    for hk in range(2):
                        for jj in range(3):
                            sbk = 4
                            mm(PB[sbk][:], kst[xi][:, hk, jj * 128:(jj + 1) * 128], ksTt[:, hk * 512:(hk + 1) * 512], True, True,
                               [kst[xi], ksTt], [PB[sbk]])
                            act(pt[jj][:], PB[sbk][:], AF.Exp, [PB[sbk], swab], [pt[jj]], scale=0.125,
                                bias=swab[:, qg * 3 + jj:qg * 3 + jj + 1])
                            if jj != 1:
                                mi = 0 if jj == 0 else 1
                                p3 = pt[jj][:].rearrange("p (g t) -> p g t", g=4)
                                tt("dve", p3, p3, bandm[:, mi, :].unsqueeze(1).to_broadcast([128, 4, 128]), ALU.mult,
                                   [pt[jj], bandm], [pt[jj]])
                        for jj in range(3):
                            mm(PB[5][:], vst[xi][:, jj, hk * 128:(hk + 1) * 128], pt[jj][:], jj == 0, False, [vst[xi], pt[jj]], [PB[5]])
                        mm(PB[5][:], e128[0:1, :], sinkrow[0:1, hk * 512:(hk + 1) * 512], False, True, [e128, sinkrow], [PB[5]])
                        act(rrS[:], PB[5][64:128, :], AF.Ln, [PB[5]], [rrS])
                        act(rrS[:], rrS[:], AF.Exp, [rrS], [rrS], scale=-1.0)
                        tt("dve", ost[:, hk * 512:(hk + 1) * 512], PB[5][0:64, :], rrS[:], ALU.mult, [PB[5], rrS], [ost])
                    dma("pool", OS_d[j][:, :, q * 128:(q + 1) * 128].rearrange("c (two d) t -> d (c two) t", two=2),
                        ost[:].rearrange("p (h t) -> p h t", h=8), [ost], [])

                def B3b(bi):
                    j, q = blocksB[bi]
                    xi = bi % 2
                    for h in range(4):
                        mm(PB[6][:, 0:256], qmT[xi][:, h, :], KmT[j][:, h, :], True, True, [qmT[xi], KmT[j]], [PB[6]])
                        act(pm[:], PB[6][:, 0:256], AF.Exp, [PB[6]], [pm, rsum], scale=128 ** -0.5, accum_out=rsum[:, h:h + 1])
                        transposes([pm[:, c * 128:(c + 1) * 128] for c in range(2)], 128, 6, pmT[:], pmT, [pm], slot0=4)
                        for c in range(2):
                            mm(PB[7][:, h * 128:(h + 1) * 128], pmT[:, c, :], Vm[j][:, c, h * 128:(h + 1) * 128], c == 0, c == 1,
                               [pmT, Vm[j]], [PB[7]])
                    recip(rrec[:], rsum[:], [rsum], [rrec])
                    tt("dve", om[:], PB[7][:].rearrange("p (h d) -> p h d", h=4), rrec[:].unsqueeze(2).to_broadcast([128, 4, 128]),
                       ALU.mult, [PB[7], rrec], [om])
                    transposes([om[:, h, :] for h in range(4)], 128, 6, omT[xi][:], omT[xi], [om], slot0=4)
                    dma("pool", OM_d[j][:, :, q * 128:(q + 1) * 128].rearrange("c p t -> p c t"), omT[xi][:], [omT[xi]], [])

                pipeline(len(blocksB), [[B01], [B2a, B2bc], [B3a, B3b]])
        P.barrier()
        with contextlib.ExitStack() as s2:
            SMAX = max(j["S"] for j in jobs)
            KBMAX = SMAX // 128
            KTh = [sbuf(s2, f"KTh{i}", [96, SMAX], BF16) for i in range(2)]
            Vh = [sbuf(s2, f"Vh{i}", [128, KBMAX, 128], BF16) for i in range(2)]
            QTh = [sbuf(s2, f"QTh{i}", [96, 2048], BF16) for i in range(2)]
            ptc = [sbuf(s2, f"ptc{i}", [128, 512], BF16) for i in range(4)]
            otsb = [sbuf(s2, f"otsb{i}", [64, 512], BF16) for i in range(2)]
            otc = [sbuf(s2, f"otc{i}", [128, 512], F32) for i in range(4)]
            rrC = sbuf(s2, "rrC", [64, 512], F32)
            sc_mla = 96 ** -0.5
            it = 0
            oc = 0
            for j in range(2):
                S, KB, NQ = jobs[j]["S"], jobs[j]["KB"], jobs[j]["NQ"]
                ntok = NQ * 128
                for g0 in range(0, ntok, 2048):
                    G = min(2048, ntok - g0)
                    qtiles = [(o, min(512, G - o)) for o in range(0, G, 512)]
                    for h in range(8):
                        bi = it % 2
                        it += 1
                        dma("sp", KTh[bi][:, 0:S], KT_d[j][h], [], [KTh[bi]])
                        dma("sp", Vh[bi][:, 0:KB, :], V_d[j][h], [], [Vh[bi]])
                        dma("sp", QTh[bi][:, 0:G], QT_d[j][h][:, g0:g0 + G], [], [QTh[bi]])
                        steps = [(kb, ti) for kb in range(KB) for ti in range(len(qtiles))]
                        ns = len(steps)
                        for i in range(ns + 2):
                            if i < ns:
                                kb, ti = steps[i]
                                o, n = qtiles[ti]
                                sbk = 4 + i % 3
                                mm(PB[sbk][:, 0:n], KTh[bi][:, kb * 128:(kb + 1) * 128], QTh[bi][:, o:o + n], True, True,
                                   [KTh[bi], QTh[bi]], [PB[sbk]])
                                act(ptc[i % 4][:, 0:n], PB[sbk][:, 0:n], AF.Exp, [PB[sbk]], [ptc[i % 4]], scale=sc_mla)
                            if i >= 2:
                                kb, ti = steps[i - 2]
                                o, n = qtiles[ti]
                                mm(PB[ti][:, 0:n], Vh[bi][:, kb, :], ptc[(i - 2) % 4][:, 0:n], kb == 0, kb == KB - 1,
                                   [Vh[bi], ptc[(i - 2) % 4]], [PB[ti]])
                        for ti, (o, n) in enumerate(qtiles):
                            cp("dve", otc[ti][:, 0:n], PB[ti][:, 0:n], [PB[ti]], [otc[ti]])
                        for ti, (o, n) in enumerate(qtiles):
                            ob = otsb[oc % 2]
                            oc += 1
                            norm_rep(otc[ti], otc[ti][0:64, 0:n], otc[ti][64:128, 0:n], n, ob[:, 0:n], ob, rrC)
                            dma("pool", OT_d[j][h // 2, (h % 2) * 64:(h % 2) * 64 + 64, g0 + o:g0 + o + n], ob[:, 0:n], [ob], [])
        P.barrier()
        with contextlib.ExitStack() as s3:
            g_mixD = sbuf(s3, "g_mixD", [128, 8], F32)
            dma("sp", g_mixD[:], g_mix_d, [], [g_mixD])
            w_g = sbuf(s3, "w_g", [128, 8, 3 * D], BF16)
            w_o3 = [sbuf(s3, f"w_o3_{b_}", [128, 4, D], BF16) for b_ in range(3)]
            w_out = sbuf(s3, "w_out", [128, 8, D], BF16)
            with contextlib.ExitStack() as sL:
                stageD = [sbuf(sL, "stageD%d" % i_, [128, 3 * D], F32) for i_ in range(2)]
                load_cols(w_g, 0, w_in_d, 1952, 3 * D, 8, stageD, g_mixD)
                for b_, wd in enumerate((w_omla_d, w_oswa_d, w_omem_d)):
                    load_weight(w_o3[b_], wd, 4, D, stageD)
                load_weight(w_out, w_out_d, 8, D, stageD)
            P.barrier()
            xtd = [sbuf(s3, f"xtd{i}", [128, D], F32) for i in range(3)]
            hbD = sbuf(s3, "hbD", [128, D], BF16)
            ssq0 = sbuf(s3, "ssq0", [128, 8], F32)
            rstd0 = sbuf(s3, "rstd0", [128, 8], F32)
            tn0 = sbuf(s3, "tn0", [128, 8], F32)
            tnD = sbuf(s3, "tnD", [128, 8], F32)
            hTD = [sbuf(s3, f"hTD{i}", [128, 8, 128], BF16) for i in range(2)]
            o3 = [[sbuf(s3, f"o3_{b_}_{i}", [128, 4, 128], BF16) for i in range(2)] for b_ in range(3)]
            gbufD = [sbuf(s3, f"gbufD{i}", [128, 512], F32) for i in range(2)]
            maccD = [sbuf(s3, f"maccD{i}", [128, 512], F32) for i in range(2)]
            tbD = [sbuf(s3, f"tbD{i}", [128, 512], F32) for i in range(2)]
            merged = [sbuf(s3, f"merged{i}", [128, D], BF16) for i in range(2)]
            mT = sbuf(s3, "mT", [128, 8, 128], BF16)
            xn = [sbuf(s3, f"xn{i}", [128, D], F32) for i in range(2)]
            ssqD = sbuf(s3, "ssqD", [128, 8], F32)
            rstdD = sbuf(s3, "rstdD", [128, 8], F32)
            h2b = sbuf(s3, "h2b", [128, D], BF16)
            h2T = [sbuf(s3, f"h2T{i}", [128, 8, 128], BF16) for i in range(2)]
            blocksD = [(j, q) for j in range(2) for q in range(jobs[j]["NQ"])]
            o3_d = (OT_d, OS_d, OM_d)

            def D0(bi):
                j, q = blocksD[bi]
                qoff = jobs[j]["qoff"]
                x_t = xtd[bi % 3]
                dma("sp", x_t[:], xkv[j][(qoff + q) * 128:(qoff + q + 1) * 128, :], [], [x_t])
                for b_ in range(3):
                    dma("sp", o3[b_][bi % 2][:], o3_d[b_][j][:, :, q * 128:(q + 1) * 128].rearrange("c p t -> p c t"), [], [o3[b_][bi % 2]])
                dma("sp", hTD[bi % 2][:], HT_d[j][:, :, q * 128:(q + 1) * 128].rearrange("c p t -> p c t"), [], [hTD[bi % 2]])

            def mk_Dm(nh):
                gb, pb_ = 1 + 2 * nh, 2 + 2 * nh

                def Dm(bi):
                    xi = bi % 2
                    for b_ in range(3):
                        proj(gb, b_ * D + nh * 512, 512, hTD[xi], w_g)
                        act(gbufD[nh][:], PB[gb][:], AF.Sigmoid, [PB[gb]], [gbufD[nh]])
                        for c in range(4):
                            mm(PB[pb_][:], o3[b_][xi][:, c, :], w_o3[b_][:, c, nh * 512:(nh + 1) * 512], c == 0, c == 3,
                               [o3[b_][xi], w_o3[b_]], [PB[pb_]])
                        if b_ == 0:
                            tt("dve", maccD[nh][:], PB[pb_][:], gbufD[nh][:], ALU.mult, [PB[pb_], gbufD[nh]], [maccD[nh]])
                        else:
                            tt("dve", tbD[nh][:], PB[pb_][:], gbufD[nh][:], ALU.mult, [PB[pb_], gbufD[nh]], [tbD[nh]])
                            if b_ == 1:
                                tt("pool", maccD[nh][:], maccD[nh][:], tbD[nh][:], ALU.add, [maccD[nh], tbD[nh]], [maccD[nh]])
                            else:
                                tt("pool", merged[xi][:, nh * 512:(nh + 1) * 512], maccD[nh][:], tbD[nh][:], ALU.add,
                                   [maccD[nh], tbD[nh]], [merged[xi]])
                return Dm

            def D2(bi):
                j, q = blocksD[bi]
                xi = bi % 2
                qg = jobs[j]["qg0"] + q
                x_t = xtd[bi % 3]
                transposes([merged[xi][:, c * 128:(c + 1) * 128] for c in range(8)], 128, 5, mT[:], mT, [merged[xi]], evac="act")
                for nh in range(2):
                    for c in range(8):
                        mm(PB[6 + nh][:], mT[:, c, :], w_out[:, c, nh * 512:(nh + 1) * 512], c == 0, c == 7, [mT, w_out], [PB[6 + nh]])
                    tt("dve", xn[xi][:, nh * 512:(nh + 1) * 512], PB[6 + nh][:], x_t[:, nh * 512:(nh + 1) * 512], ALU.add,
                       [PB[6 + nh], x_t], [xn[xi]])
                dma("pool", XN_d[j][q * 128:(q + 1) * 128, :], xn[xi][:], [xn[xi]], [])
                front(xn[xi], h2b, ssqD, rstdD, 5, h2T[xi], extra_scale=hval[:, qg:qg + 1], tn=tnD)
                dma("pool", H2_d[j][:, :, 1 + q * 128:1 + (q + 1) * 128].rearrange("c p t -> p c t"), h2T[xi][:], [h2T[xi]], [])

            pipeline(len(blocksD), [[D0], [mk_Dm(0), mk_Dm(1)], [D2]])
        P.barrier()
        with contextlib.ExitStack() as s4:
            g_ffn = sbuf(s4, "g_ffn", [128, 8], F32)
            convw = sbuf(s4, "convw", [128, NCH_FF, 3], F32)
            convb = sbuf(s4, "convb", [128, NCH_FF], F32)
            dma("sp", g_ffn[:], g_ffn_d, [], [g_ffn])
            dma("sp", convw[:], convw_d, [], [convw])
            dma("sp", convb[:], convb_d, [], [convb])
            stageE = sbuf(s4, "stageE", [128, 1408], F32)
            w_up = sbuf(s4, "w_up", [128, 8, 2 * D_FF], BF16)
            w_down = sbuf(s4, "w_down", [128, 22, D], BF16)
            load_weight(w_up, w_up_d, 8, 2 * D_FF, stageE, gain=g_ffn, cpiece=1408)
            load_weight(w_down, w_down_d, 22, D, stageE)
            h2t = sbuf(s4, "h2t", [128, 8, FT + 2], BF16)
            xnt = [sbuf(s4, f"xnt{i}", [128, D], F32) for i in range(2)]
            gT = sbuf(s4, "gT", [128, 22, FT], BF16)
            ub = [sbuf(s4, f"ub{i}", [128, FT + 2], F32) for i in range(2)]
            cc = [sbuf(s4, f"cc{i}", [128, FT], F32) for i in range(4)]
            yo = sbuf(s4, "yo", [128, D], F32)
            ucnt = 0
            ycnt = 0
            for j in range(2):
                f0, fn = jobs[j]["f0"], jobs[j]["fn"]
                for i in range(fn * 128 // FT):
                    tok0 = f0 * 128 + i * FT
                    dma("sp", h2t[:], H2_d[j][:, :, tok0:tok0 + FT + 2].rearrange("c p t -> p c t"),
                        [DB("H2", j), DB("H2pad", j, 0), DB("H2pad", j, jobs[j]["NQ"] * 128 + 1)], [h2t])
                    for i2 in range(22):
                        for which in range(2):
                            ch = i2 + 22 * which
                            u = ub[ucnt % 2]
                            bm, bh = 1 + 2 * (ucnt % 2), 2 + 2 * (ucnt % 2)
                            ucnt += 1
                            for k in range(8):
                                mm(PB[bm][:, 0:FT], w_up[:, k, ch * 128:(ch + 1) * 128], h2t[:, k, 1:FT + 1], k == 0, k == 7, [w_up, h2t], [PB[bm]])
                            for k in range(8):
                                mm(PB[bh][:, 0:2], w_up[:, k, ch * 128:(ch + 1) * 128], h2t[:, k, 0:FT + 2:FT + 1], k == 0, k == 7, [w_up, h2t], [PB[bh]])
                            cp("act", u[:, 1:FT + 1], PB[bm][:, 0:FT], [PB[bm]], [u])
                            cp("act", u[:, 0:FT + 2:FT + 1], PB[bh][:, 0:2], [PB[bh]], [u])
                            c_ = cc[2 * (i2 % 2) + which]
                            ts("dve", c_[:], u[:, 1:FT + 1], convw[:, ch, 1:2], convb[:, ch:ch + 1], ALU.mult, ALU.add, [u, convw, convb], [c_])
                            stt("dve", c_[:], u[:, 0:FT], convw[:, ch, 0:1], c_[:], ALU.mult, ALU.add, [u, convw, c_], [c_])
                            stt("dve", c_[:], u[:, 2:FT + 2], convw[:, ch, 2:3], c_[:], ALU.mult, ALU.add, [u, convw, c_], [c_])
                        ca, cv = cc[2 * (i2 % 2)], cc[2 * (i2 % 2) + 1]
                        act(ca[:], ca[:], AF.Silu, [ca], [ca])
                        tt("pool", gT[:, i2, :], ca[:], cv[:], ALU.mult, [ca, cv], [gT])
                    for tb in range(FT // 128):
                        xi = ycnt % 2
                        ycnt += 1
                        row0 = tok0 + tb * 128
                        dma("sp", xnt[xi][:], XN_d[j][row0:row0 + 128, :], [DB("XN", j)], [xnt[xi]])
                        for nh in range(2):
                            for i2 in range(22):
                                mm(PB[5 + nh][:], gT[:, i2, tb * 128:(tb + 1) * 128], w_down[:, i2, nh * 512:(nh + 1) * 512], i2 == 0, i2 == 21,
                                   [gT, w_down], [PB[5 + nh]])
                            tt("dve", yo[:, nh * 512:(nh + 1) * 512], PB[5 + nh][:], xnt[xi][:, nh * 512:(nh + 1) * 512], ALU.add,
                               [PB[5 + nh], xnt[xi]], [yo])
                        orow = row0 - f0 * 128
                        dma("pool", y_d[j][orow:orow + 128, :], yo[:], [yo], [])
        P.emit()
    return nc


ROPE_THETA = 10000.0


def _rope_tab(pos, dim):
    inv = (1.0 / (ROPE_THETA ** (np.arange(0, dim, 2, dtype=np.float32) / np.float32(dim)))).astype(np.float32)
    ang = pos.astype(np.float32)[:, None] * inv[None, :]
    return np.cos(ang).astype(np.float32), np.sin(ang).astype(np.float32)


def make_core_inputs(c, inp, SP, SS):
    QS = SS // 4
    f32 = np.float32
    s = c // 4
    t0 = QS * (c % 4)
    shift = t0 - 256
    idx1 = (np.arange(SS) + shift) % SS
    pos = [np.arange(SP), idx1]
    m = {}
    m["xkv0"] = np.ascontiguousarray(inp["x_prompt"][c], dtype=f32)
    m["xkv1"] = np.ascontiguousarray(np.asarray(inp["x_sample"][s])[idx1], dtype=f32)
    m["mem0"] = np.ascontiguousarray(inp["mem_prompt"][c], dtype=f32)
    m["mem1"] = np.ascontiguousarray(inp["mem_sample"][s], dtype=f32)
    for j in range(2):
        cm, sm = _rope_tab(pos[j], 32)
        cs, ss_ = _rope_tab(pos[j], 64)
        tab = np.concatenate([cm, sm, cs, ss_], axis=1)
        m[f"rope{j}"] = np.ascontiguousarray(tab.reshape(-1, 128, 96))
    NQ0, NQ1 = SP // 128, QS // 128 + 2
    hv = np.ones((NQ0 + NQ1,), f32)
    sw = np.zeros((NQ0 + NQ1, 3), f32)
    sw[0, 0] = NEG
    sw[NQ0 - 1, 2] = NEG
    nblk = SS // 128
    for q in range(NQ1):
        p = (t0 // 128) - 1 + q
        if p - 1 < 0 or p - 1 > nblk - 1:
            sw[NQ0 + q, 0] = NEG
        if p + 1 > nblk - 1 or p + 1 < 0:
            sw[NQ0 + q, 2] = NEG
        if p < 0 or p > nblk - 1:
            hv[NQ0 + q] = 0.0
    m["hval"] = np.ascontiguousarray(np.broadcast_to(hv[None, :], (128, NQ0 + NQ1)))
    m["swab"] = np.ascontiguousarray(np.broadcast_to(sw.reshape(1, -1), (128, (NQ0 + NQ1) * 3)))
    k = np.arange(128)[:, None]
    qq = np.arange(128)[None, :]
    band = np.stack([(k >= qq), (k <= qq)], axis=1).astype(f32)
    m["bandm"] = band.astype(ml_dtypes.bfloat16)
    m["ident"] = np.eye(128, dtype=f32).astype(ml_dtypes.bfloat16)

    def colvec(v, n):
        return np.ascontiguousarray(np.asarray(v, f32).reshape(n, 128).T)

    m["w_in"] = np.ascontiguousarray(inp["w_in"][0], dtype=f32)
    m["g_mix"] = colvec(inp["g_mix"][0], 8)
    m["g_mem"] = colvec(inp["g_mem"][0], 8)
    m["g_ffn"] = colvec(inp["g_ffn"][0], 8)
    m["q_a_norm"] = colvec(inp["q_a_norm"][0], 3)
    m["kv_a_norm"] = colvec(inp["kv_a_norm"][0], 2)
    m["w_q_b"] = np.ascontiguousarray(inp["w_q_b"][0], dtype=f32)
    m["w_kv_b"] = np.ascontiguousarray(inp["w_kv_b"][0], dtype=f32)
    for nm in ("g_q_mla", "g_k_mla", "g_q_swa", "g_k_swa", "g_q_mem", "g_k_mem", "swa_sink"):
        m[nm] = np.ascontiguousarray(np.asarray(inp[nm], f32).reshape(1, -1))
    for nm in ("w_mem_kv", "w_o_mla", "w_o_swa", "w_o_mem", "w_out", "w_up", "w_down"):
        m[nm] = np.ascontiguousarray(inp[nm][0], dtype=f32)
    m["conv_w"] = np.ascontiguousarray(np.asarray(inp["conv_w"][0], f32).T.reshape(NCH_FF, 128, 3).transpose(1, 0, 2))
    m["conv_b"] = colvec(inp["conv_b"][0], NCH_FF)
    return m


def run(inp, SP, SS, FT, n_cores=8, runner=None):
    nc = build_program(SP, SS, FT)
    in_maps = [make_core_inputs(c, inp, SP, SS) for c in range(n_cores)]
    if runner is None:
        res = run_bass_kernel_spmd(nc, in_maps, core_ids=list(range(n_cores))).results
    else:
        res = runner(nc, in_maps)
    QS = SS // 4
    yp = np.stack([np.asarray(res[c]["y0"], np.float32) for c in range(n_cores)], axis=0)
    ys = np.zeros((n_cores // 4, SS, D), np.float32)
    for c in range(n_cores):
        ys[c // 4, QS * (c % 4):QS * (c % 4 + 1)] = np.asarray(res[c]["y1"], np.float32)
    return yp, ys


def kernel(**inputs):
    inp = {k: np.asarray(v) for k, v in inputs.items()}
    yp, ys = run(inp, 8192, 16384, 512)
    return (yp, ys)
```
